# Optimizing a Trainium2 kernel written in Bass

```python
import math
import jax, jax.numpy as jnp
from jax import lax
import numpy as np

D_MODEL = 1024
BATCH = 16
SEQ = 2048
DEPTH = 1

GRID_W = 64
CTX_LEN = 256

N_HEADS = 8
N_KV_HEADS = 2
HEAD_DIM = 128
ATTN_DIM = N_HEADS * HEAD_DIM
KV_DIM = N_KV_HEADS * HEAD_DIM
Q_BLOCK = 128
ROPE_THETA = 10000.0
ROPE_AXIS_DIM = HEAD_DIM // 2

SSM_DIM = 512
SSM_GROUP = 16
N_SSM_GROUPS = SSM_DIM // SSM_GROUP
SSM_STATE = 64
N_DIRECTIONS = 2
DT_MIN = 1e-3
DT_MAX = 1e-1

N_BRANCHES = 2
Q_END = ATTN_DIM
K_END = Q_END + KV_DIM
V_END = K_END + KV_DIM
U_END = V_END + SSM_DIM
D_IN = U_END + N_BRANCHES * D_MODEL

D_FF = 2816
CONV_W = 3

N_MOD = 6
EPS = 1e-6

kernel_name = 'hybrid_s5_gqa_convffn_dit_prefix'


def _rmsnorm(x, w):
    xf = x.astype(jnp.float32)
    y = xf * lax.rsqrt(jnp.mean(xf * xf, axis=-1, keepdims=True) + EPS)
    return (y * w.astype(jnp.float32)).astype(x.dtype)


def _modulate(h, shift, scale):
    return h * (1.0 + scale) + shift


def _heads(t, n_heads):
    b, l, _ = t.shape
    return t.reshape(b, l, n_heads, HEAD_DIM).transpose(0, 2, 1, 3)


def _tokens(t):
    b, n, l, dh = t.shape
    return t.transpose(0, 2, 1, 3).reshape(b, l, n * dh)


def _rope_tables(rows, cols):
    inv_freq = ROPE_THETA ** (-jnp.arange(0, ROPE_AXIS_DIM, 2, dtype=jnp.float32) / ROPE_AXIS_DIM)
    ang = jnp.concatenate([rows[:, None] * inv_freq, cols[:, None] * inv_freq], axis=-1)
    return jnp.cos(ang), jnp.sin(ang)


def _apply_rope(t, cos, sin):
    tf = t.astype(jnp.float32).reshape(*t.shape[:-1], HEAD_DIM // 2, 2)
    t0, t1 = tf[..., 0], tf[..., 1]
    out = jnp.stack([t0 * cos - t1 * sin, t0 * sin + t1 * cos], axis=-1)
    return out.reshape(t.shape).astype(t.dtype)


def _gqa_sweep(q, k, v):
    b, h, lq, dh = q.shape
    rep = h // N_KV_HEADS
    nblk = lq // Q_BLOCK
    qb = jnp.moveaxis(q.reshape(b, N_KV_HEADS, rep, nblk, Q_BLOCK, dh), 3, 0)
    scale = 1.0 / math.sqrt(HEAD_DIM)

    def one_block(q_blk):
        s = jnp.einsum('bkgqd,bktd->bkgqt', q_blk, k).astype(jnp.float32) * scale
        p = jax.nn.softmax(s, axis=-1).astype(v.dtype)
        return jnp.einsum('bkgqt,bktd->bkgqd', p, v)

    o = lax.map(one_block, qb)
    return jnp.moveaxis(o, 0, 3).reshape(b, h, lq, dh)


def _zoh(lam_re, lam_im, log_dt, b_re, b_im):
    lam_re = lam_re.astype(jnp.float32)
    lam_im = lam_im.astype(jnp.float32)
    b_re = b_re.astype(jnp.float32)
    b_im = b_im.astype(jnp.float32)
    dt = jnp.exp(log_dt.astype(jnp.float32))[:, None]
    mag = jnp.exp(lam_re * dt)
    ang = lam_im * dt
    abar_re = mag * jnp.cos(ang)
    abar_im = mag * jnp.sin(ang)
    den = lam_re * lam_re + lam_im * lam_im
    nr = abar_re - 1.0
    ni = abar_im
    f_re = (nr * lam_re + ni * lam_im) / den
    f_im = (ni * lam_re - nr * lam_im) / den
    bbar_re = f_re[..., None] * b_re - f_im[..., None] * b_im
    bbar_im = f_re[..., None] * b_im + f_im[..., None] * b_re
    return abar_re, abar_im, bbar_re, bbar_im


def _ssm_combine(first, second):
    a_re, a_im, x_re, x_im = first
    b_re, b_im, y_re, y_im = second
    return (a_re * b_re - a_im * b_im,
            a_re * b_im + a_im * b_re,
            b_re * x_re - b_im * x_im + y_re,
            b_re * x_im + b_im * x_re + y_im)


def _s5_direction(u_ctx, u_lat, lam_re, lam_im, log_dt, b_re, b_im, reverse):
    abar_re, abar_im, bbar_re, bbar_im = _zoh(lam_re, lam_im, log_dt, b_re, b_im)

    def drive(u):
        return (jnp.einsum('blgp,gnp->blgn', u, bbar_re),
                jnp.einsum('blgp,gnp->blgn', u, bbar_im))

    def scan(bu_re, bu_im):
        l = bu_re.shape[1]
        a_re = jnp.broadcast_to(abar_re, (1, l) + abar_re.shape)
        a_im = jnp.broadcast_to(abar_im, (1, l) + abar_im.shape)
        _, _, s_re, s_im = lax.associative_scan(
            _ssm_combine, (a_re, a_im, bu_re, bu_im), reverse=reverse, axis=1)
        return s_re, s_im

    ctx_end = 0 if reverse else -1
    lat_start = -1 if reverse else 0
    sc_re, sc_im = scan(*drive(u_ctx))
    s0_re, s0_im = sc_re[:, ctx_end], sc_im[:, ctx_end]
    bl_re, bl_im = drive(u_lat)
    bl_re = bl_re.at[:, lat_start].add(abar_re * s0_re - abar_im * s0_im)
    bl_im = bl_im.at[:, lat_start].add(abar_re * s0_im + abar_im * s0_re)
    sl_re, sl_im = scan(bl_re, bl_im)
    return sl_re, sl_im, sc_re, sc_im


def _s5_readout(s_re, s_im, c_re, c_im):
    y = (jnp.einsum('blgn,gpn->blgp', s_re, c_re.astype(jnp.float32))
         - jnp.einsum('blgn,gpn->blgp', s_im, c_im.astype(jnp.float32)))
    return y.reshape(y.shape[0], y.shape[1], SSM_DIM)


def _s5_bidirectional(u_lat, u_ctx, lam_re, lam_im, log_dt, b_re, b_im, c_re, c_im, d_skip, with_ctx_out):
    b, l, _ = u_lat.shape
    lc = u_ctx.shape[1]
    ul = u_lat.astype(jnp.float32).reshape(b, l, N_SSM_GROUPS, SSM_GROUP)
    uc = u_ctx.astype(jnp.float32).reshape(b, lc, N_SSM_GROUPS, SSM_GROUP)
    d = d_skip.astype(jnp.float32)
    y_lat = u_lat.astype(jnp.float32) * d
    ctx_terms = [u_ctx.astype(jnp.float32) * d] if with_ctx_out else []
    for direction in range(N_DIRECTIONS):
        sl_re, sl_im, sc_re, sc_im = _s5_direction(
            uc, ul, lam_re[direction], lam_im[direction], log_dt[direction],
            b_re[direction], b_im[direction], reverse=(direction == 1))
        y_lat = y_lat + _s5_readout(sl_re, sl_im, c_re[direction], c_im[direction])
        if with_ctx_out:
            ctx_terms.append(_s5_readout(sc_re, sc_im, c_re[direction], c_im[direction]))
    y_ctx = sum(ctx_terms).astype(u_ctx.dtype) if with_ctx_out else None
    return y_lat.astype(u_lat.dtype), y_ctx


def _merge_branches(attn_tok, ssm_tok, g, w_attn_br, w_glu, w_out):
    p_attn = attn_tok @ w_attn_br
    glu_a, glu_b = jnp.split(jax.nn.gelu(ssm_tok) @ w_glu, 2, axis=-1)
    p_ssm = glu_a * jax.nn.sigmoid(glu_b)
    g_attn, g_ssm = jnp.split(g, N_BRANCHES, axis=-1)
    return (jax.nn.sigmoid(g_attn) * p_attn + jax.nn.sigmoid(g_ssm) * p_ssm) @ w_out


def _conv_ffn(h, w_up, conv_w, conv_b, w_down):
    z = h @ w_up
    l = z.shape[1]
    pad = CONV_W // 2
    zp = jnp.pad(z, ((0, 0), (pad, pad), (0, 0)))
    z = sum(zp[:, j:j + l] * conv_w[j] for j in range(CONV_W)) + conv_b
    val, gate = jnp.split(z, 2, axis=-1)
    return (jax.nn.silu(gate) * val) @ w_down


def setup_inputs(seed: int = 0) -> dict:
    key = jax.random.key(seed)
    ks = jax.random.split(key, 32)
    f32 = jnp.float32
    G, N, P = N_SSM_GROUPS, SSM_STATE, SSM_GROUP

    def nrm(k, shape, scale):
        return jax.random.normal(k, shape, f32) * scale

    n_idx = jnp.arange(N, dtype=f32)
    lam_re = -0.5 + nrm(ks[10], (DEPTH, N_DIRECTIONS, G, N), 0.01)
    lam_im = math.pi * n_idx + nrm(ks[11], (DEPTH, N_DIRECTIONS, G, N), 0.01)
    log_dt = jax.random.uniform(ks[12], (DEPTH, N_DIRECTIONS, G), f32,
                                math.log(DT_MIN), math.log(DT_MAX))
    return {
        'x': nrm(ks[0], (BATCH, SEQ, D_MODEL), 1.0),
        'c': nrm(ks[1], (BATCH, D_MODEL), 1.0),
        'ctx': nrm(ks[2], (BATCH, CTX_LEN, D_MODEL), 1.0),
        'c_ctx': nrm(ks[3], (D_MODEL,), 1.0),
        'w_mod': nrm(ks[4], (DEPTH, D_MODEL, N_MOD * D_MODEL), 0.5 * D_MODEL ** -0.5),
        'b_mod': nrm(ks[5], (DEPTH, N_MOD * D_MODEL), 0.02),
        'norm1_w': 1.0 + nrm(ks[6], (DEPTH, D_MODEL), 0.02),
        'norm2_w': 1.0 + nrm(ks[7], (DEPTH, D_MODEL), 0.02),
        'w_in': nrm(ks[8], (DEPTH, D_MODEL, D_IN), D_MODEL ** -0.5),
        'q_norm_w': 1.0 + nrm(ks[9], (DEPTH, HEAD_DIM), 0.02),
        'k_norm_w': 1.0 + nrm(ks[13], (DEPTH, HEAD_DIM), 0.02),
        'w_attn_br': nrm(ks[14], (DEPTH, ATTN_DIM, D_MODEL), ATTN_DIM ** -0.5),
        'ssm_lambda_re': lam_re,
        'ssm_lambda_im': lam_im,
        'ssm_log_dt': log_dt,
        'ssm_b_re': nrm(ks[15], (DEPTH, N_DIRECTIONS, G, N, P), (2 * P) ** -0.5),
        'ssm_b_im': nrm(ks[16], (DEPTH, N_DIRECTIONS, G, N, P), (2 * P) ** -0.5),
        'ssm_c_re': nrm(ks[17], (DEPTH, N_DIRECTIONS, G, P, N), (2 * N) ** -0.5),
        'ssm_c_im': nrm(ks[18], (DEPTH, N_DIRECTIONS, G, P, N), (2 * N) ** -0.5),
        'ssm_d': nrm(ks[19], (DEPTH, SSM_DIM), 1.0),
        'w_glu': nrm(ks[20], (DEPTH, SSM_DIM, 2 * D_MODEL), SSM_DIM ** -0.5),
        'w_out': nrm(ks[21], (DEPTH, D_MODEL, D_MODEL), D_MODEL ** -0.5),
        'w_up': nrm(ks[22], (DEPTH, D_MODEL, 2 * D_FF), D_MODEL ** -0.5),
        'conv_w': nrm(ks[23], (DEPTH, CONV_W, 2 * D_FF), CONV_W ** -0.5),
        'conv_b': nrm(ks[24], (DEPTH, 2 * D_FF), 0.02),
        'w_down': nrm(ks[25], (DEPTH, D_FF, D_MODEL), D_FF ** -0.5),
        'final_norm_w': 1.0 + nrm(ks[26], (D_MODEL,), 0.02),
    }


def reference(x, c, ctx, c_ctx, w_mod, b_mod, norm1_w, norm2_w, w_in, q_norm_w, k_norm_w,
              w_attn_br, ssm_lambda_re, ssm_lambda_im, ssm_log_dt, ssm_b_re, ssm_b_im,
              ssm_c_re, ssm_c_im, ssm_d, w_glu, w_out, w_up, conv_w, conv_b, w_down,
              final_norm_w):
    l = x.shape[1]
    ROWS = l // GRID_W
    rows = jnp.repeat(jnp.arange(ROWS, dtype=jnp.float32), GRID_W)
    cols = jnp.tile(jnp.arange(GRID_W, dtype=jnp.float32), ROWS)
    cos, sin = _rope_tables(rows, cols)

    for layer in range(DEPTH):
        update_ctx = layer < DEPTH - 1
        mod = jax.nn.silu(c) @ w_mod[layer] + b_mod[layer]
        mod_c = jax.nn.silu(c_ctx) @ w_mod[layer] + b_mod[layer]
        sh1, sc1, g1, sh2, sc2, g2 = jnp.split(mod[:, None, :], N_MOD, axis=-1)
        csh1, csc1, cg1, csh2, csc2, cg2 = jnp.split(mod_c, N_MOD, axis=-1)
        w_in_l = w_in[layer]

        h = _modulate(_rmsnorm(x, norm1_w[layer]), sh1, sc1)
        hc = _modulate(_rmsnorm(ctx, norm1_w[layer]), csh1, csc1)
        q, k, v, u, g = jnp.split(h @ w_in_l, (Q_END, K_END, V_END, U_END), axis=-1)
        kc, vc, uc = jnp.split(hc @ w_in_l[:, Q_END:U_END], (KV_DIM, 2 * KV_DIM), axis=-1)

        q = _apply_rope(_rmsnorm(_heads(q, N_HEADS), q_norm_w[layer]), cos, sin)
        k = _apply_rope(_rmsnorm(_heads(k, N_KV_HEADS), k_norm_w[layer]), cos, sin)
        v = _heads(v, N_KV_HEADS)
        kc = _rmsnorm(_heads(kc, N_KV_HEADS), k_norm_w[layer])
        vc = _heads(vc, N_KV_HEADS)
        attn_tok = _tokens(_gqa_sweep(q, jnp.concatenate([kc, k], axis=2),
                                      jnp.concatenate([vc, v], axis=2)))

        y_ssm, y_ssm_c = _s5_bidirectional(
            u, uc, ssm_lambda_re[layer], ssm_lambda_im[layer], ssm_log_dt[layer],
            ssm_b_re[layer], ssm_b_im[layer], ssm_c_re[layer], ssm_c_im[layer], ssm_d[layer],
            with_ctx_out=update_ctx)

        x_mix = _merge_branches(attn_tok, y_ssm, g, w_attn_br[layer], w_glu[layer], w_out[layer])
        x = x + g1 * x_mix

        h2 = _modulate(_rmsnorm(x, norm2_w[layer]), sh2, sc2)
        x = x + g2 * _conv_ffn(h2, w_up[layer], conv_w[layer], conv_b[layer], w_down[layer])

        if update_ctx:
            qc = _rmsnorm(_heads(hc @ w_in_l[:, :Q_END], N_HEADS), q_norm_w[layer])
            gc = hc @ w_in_l[:, U_END:]
            attn_c = _tokens(_gqa_sweep(qc, kc, vc))
            ctx = ctx + cg1 * _merge_branches(attn_c, y_ssm_c, gc, w_attn_br[layer],
                                              w_glu[layer], w_out[layer])
            hc2 = _modulate(_rmsnorm(ctx, norm2_w[layer]), csh2, csc2)
            ctx = ctx + cg2 * _conv_ffn(hc2, w_up[layer], conv_w[layer], conv_b[layer], w_down[layer])

    return _rmsnorm(x, final_norm_w)
```

```python
import contextlib
import math
import numpy as np
import concourse.bass as bass
import concourse.mybir as mybir
from concourse.bass_utils import run_bass_kernel_spmd

F32 = mybir.dt.float32
BF16 = mybir.dt.bfloat16
I32 = mybir.dt.int32
AF = mybir.ActivationFunctionType
ALU = mybir.AluOpType

D = 1024
DT = 8
L = 2048
LC = 256
S = L + LC
NTT = S // 128
NCH = S // 8
NH = 8
DFF = 2816
NFT = DFF // 128
EPS = 1e-6
TWO_PI = 2.0 * math.pi
PI_SAFE = 3.1415925
ENGS = ("pe", "act", "dve", "pool", "sp")


class Op:
    __slots__ = ("eng", "fn", "idx", "deps", "same", "is_dma", "inc", "sem", "val", "waits")

    def __init__(self, eng, fn, idx, is_dma):
        self.eng = eng
        self.fn = fn
        self.idx = idx
        self.deps = set()
        self.same = set()
        self.is_dma = is_dma
        self.inc = False
        self.sem = None
        self.val = 0
        self.waits = []


class Prog:
    N_DMA_SEMS = 24

    def __init__(self, nc, st):
        self.nc = nc
        self.st = st
        self.eng_sem = {e: st.enter_context(nc.semaphore("s_" + e)) for e in ENGS}
        self.dma_sems = [st.enter_context(nc.semaphore("s_dma%d" % i)) for i in range(self.N_DMA_SEMS)]
        self.dma_count = [0] * self.N_DMA_SEMS
        self.dma_rr = 0
        self.cnt = {e: 0 for e in ENGS}
        self.nsem = 0
        self._reset()

    def _reset(self):
        self.eng_ops = {e: [] for e in ENGS}
        self.last_writer = {}
        self.readers = {}
        self.all_ops = []

    def add(self, eng, fn, reads=(), writes=(), dma=False, big=False):
        op = Op(eng, fn, len(self.eng_ops[eng]), dma)
        for r in reads:
            w = self.last_writer.get(r)
            if w is not None:
                op.deps.add(w)
                if w.eng == eng and eng != "pe":
                    op.same.add(w)
        for w in writes:
            lw = self.last_writer.get(w)
            if lw is not None:
                op.deps.add(lw)
                if lw.eng == eng and eng != "pe":
                    op.same.add(lw)
            for rd in self.readers.get(w, ()):
                op.deps.add(rd)
        for r in reads:
            self.readers.setdefault(r, []).append(op)
        for w in writes:
            self.last_writer[w] = op
            self.readers[w] = []
        op.deps.discard(op)
        if big:
            op.same.clear()
        self.eng_ops[eng].append(op)
        self.all_ops.append(op)
        return op

    def pe(self, fn, reads=(), writes=()):
        return self.add("pe", fn, reads, writes)

    def act(self, fn, reads=(), writes=(), big=False):
        return self.add("act", fn, reads, writes, big=big)

    def dve(self, fn, reads=(), writes=(), big=False):
        return self.add("dve", fn, reads, writes, big=big)

    def pool(self, fn, reads=(), writes=(), big=False):
        return self.add("pool", fn, reads, writes, big=big)

    def dma(self, eng, out, in_, reads=(), writes=()):
        return self.add(eng, lambda e: e.dma_start(out=out, in_=in_), reads, writes, dma=True)

    def emit(self):
        nc = self.nc
        for op in self.all_ops:
            if op.is_dma:
                op.inc = True
            for d in op.deps:
                if d.is_dma or d.eng != op.eng or d in op.same:
                    d.inc = True
        dma_last = [None] * self.N_DMA_SEMS
        for op in self.all_ops:
            if not op.inc:
                continue
            if op.is_dma:
                k = self.dma_rr % self.N_DMA_SEMS
                self.dma_rr += 1
                prev = dma_last[k]
                if prev is not None:
                    op.deps.add(prev)
                self.dma_count[k] += 16
                op.sem = self.dma_sems[k]
                op.val = self.dma_count[k]
                dma_last[k] = op
            else:
                if self.cnt[op.eng] >= 30000:
                    self.cnt[op.eng] = 0
                    self.nsem += 1
                    self.eng_sem[op.eng] = self.st.enter_context(
                        nc.semaphore("s_%s_%d" % (op.eng, self.nsem)))
                self.cnt[op.eng] += 1
                op.sem = self.eng_sem[op.eng]
                op.val = self.cnt[op.eng]
        for e in ENGS:
            seen = {}
            for op in self.eng_ops[e]:
                need = {}
                for d in op.deps:
                    if (not d.is_dma) and d.eng == e and d not in op.same:
                        continue
                    key = id(d.sem)
                    if seen.get(key, 0) >= d.val:
                        continue
                    if key not in need or need[key][1] < d.val:
                        need[key] = (d.sem, d.val)
                for key, (s, v) in need.items():
                    seen[key] = v
                    op.waits.append((s, v))
        fin = [(self.dma_sems[k], self.dma_count[k]) for k in range(self.N_DMA_SEMS)
               if dma_last[k] is not None]
        eng_ops = self.eng_ops
        with nc.Block() as block:
            def run(e, eng):
                for op in eng_ops[e]:
                    for (s, v) in op.waits:
                        eng.wait_ge(s, v)
                    ins = op.fn(eng)
                    if op.inc:
                        ins.then_inc(op.sem, 16 if op.is_dma else 1)

            @block.tensor
            def _(eng):
                run("pe", eng)

            @block.scalar
            def _(eng):
                run("act", eng)

            @block.vector
            def _(eng):
                run("dve", eng)

            @block.gpsimd
            def _(eng):
                run("pool", eng)

            @block.sync
            def _(eng):
                run("sp", eng)
                for (s, v) in fin:
                    eng.wait_ge(s, v)
        self._reset()


class Regs:
    def __init__(self, arena, ranges):
        self.arena = arena
        self.ranges = [[a, a, b] for (a, b) in ranges]

    def take(self, shape, dt):
        esz = 2 if dt == BF16 else 4
        n = 1
        for d_ in shape[1:]:
            n *= d_
        nbytes = (n * esz + 63) // 64 * 64
        for r in self.ranges:
            if r[0] + nbytes <= r[2]:
                a0 = r[0] // 2
                r[0] += nbytes
                ap = self.arena[0:shape[0], a0:a0 + n * esz // 2]
                if dt != BF16:
                    ap = ap.bitcast(dt)
                if len(shape) > 2:
                    names = "abcde"[:len(shape) - 1]
                    pat = "p (%s) -> p %s" % (" ".join(names), " ".join(names))
                    ap = ap.rearrange(pat, **{names[i]: shape[i + 1] for i in range(len(names))})
                return ap
        raise AssertionError("arena region overflow: %s %s" % (shape, self.ranges))


def build_nc(debug=False):
    nc = bass.Bass("TRN2", target_bir_lowering=False)

    def din(name, shape, dt=F32):
        return nc.dram_tensor(name, list(shape), dt, kind="ExternalInput").ap()

    xin = din("xin", [2, S, D])
    c3T = din("c3T", [128, DT, 4])
    w_mod = din("w_mod", [128, DT, 6 * D])
    bmodT = din("bmodT", [128, 48])
    n1T = din("n1T", [128, DT])
    n2T = din("n2T", [128, DT])
    fnw_d = din("fnw_b", [128, D])
    w_in = din("w_in", [128, DT, 4096])
    qnw_d = din("qnw_b", [128, 128])
    knw_d = din("knw_b", [128, 128])
    w_ab = din("w_ab", [128, DT, DT, 128])
    w_ing = din("w_ing", [128, 16, DT, 128])
    w_glu = din("w_glu", [128, 16, 4, 128])
    w_out = din("w_out", [128, DT, D])
    w_up = din("w_up", [128, 2 * NFT, DT, 128])
    w_down = din("w_down", [128, DT, NFT, 128])
    convT_d = din("convT", [128, 4, 2 * NFT])
    lamre_d = din("lamT_re", [128, 32])
    lamim_d = din("lamT_im", [128, 32])
    ldt_d = din("ldtT", [128, 32])
    Btre_d = din("Bt_re", [128, 32, 16])
    Btim_d = din("Bt_im", [128, 32, 16])
    Ctre_d = din("Ct_re", [128, 32, 16])
    Ctim_d = din("Ct_im", [128, 32, 16])
    dcol_d = din("dcol", [128, 32])
    ident_d = din("ident", [128, 128])
    ropec_d = din("ropec", [128, 16, 64])
    ropes_d = din("ropes", [128, 16, 64])
    kexp_d = din("kexp", [128, 32, 18])
    mask_d = din("mask", [128, 2, 128])
    out = nc.dram_tensor("out", [2, L, D], F32, kind="ExternalOutput").ap()
    skind = "ExternalOutput" if debug else "Internal"
    gt_scr = nc.dram_tensor("gt_scr", [128, 32, 2, 128], BF16, kind=skind).ap()
    hb_scr = nc.dram_tensor("hb_scr", [128, 2, 32, 128], BF16, kind=skind).ap()
    kw_scr = nc.dram_tensor("kw_scr", [128, 2, 32, 128], BF16, kind=skind).ap()

    if debug:
        dbg_xmid = nc.dram_tensor("dbg_xmid", [L, D], F32, kind="ExternalOutput").ap()
    with contextlib.ExitStack() as st:
        p = Prog(nc, st)

        uid = [0]

        def dump(name, ap, reads):
            if not debug:
                return
            d_ = nc.dram_tensor("dbg_" + name, list(ap.shape), ap.dtype, kind="ExternalOutput").ap()
            p.dma("sp", d_, ap, reads=list(reads), writes=["dbg_" + name])

        def SB(stack, name, shape, dt=F32):
            if isinstance(stack, Regs):
                return stack.take(list(shape), dt)
            uid[0] += 1
            return stack.enter_context(nc.sbuf_tensor("sb%d_%s" % (uid[0], name), list(shape), dt))

        psF = [st.enter_context(nc.psum_tensor("ps%d" % i, [128, 512], F32)) for i in range(8)]

        def PF(i):
            return psF[i][:]

        def PB(i):
            return psF[i][:].bitcast(BF16)

        def PK(i):
            return "ps%d" % i

        identf = SB(st, "identf", [128, 128])
        identb = SB(st, "identb", [128, 128], BF16)
        onesb = SB(st, "onesb", [128, 128], BF16)
        modT = SB(st, "modT", [128, 48, 4])
        scale1 = SB(st, "scale1", [128, DT, 4])
        scale2 = SB(st, "scale2", [128, DT, 4])
        qnw = SB(st, "qnw", [128, 128])
        knw = SB(st, "knw", [128, 128])
        convT = SB(st, "convT", [128, 4, 2 * NFT])
        CRI = SB(st, "CRI", [128, 2, 2, 32])
        CRt = CRI[:, 0]
        CIt = CRI[:, 1]
        n1s = SB(st, "n1s", [128, DT])
        n2s = SB(st, "n2s", [128, DT])

        with contextlib.ExitStack() as s0:
            for (t, d_, k) in ((identf, ident_d, "identf"), (qnw, qnw_d, "qnw"), (knw, knw_d, "knw"),
                               (convT, convT_d, "convT"), (n1s, n1T, "n1s"), (n2s, n2T, "n2s")):
                p.dma("sp", t[:], d_, writes=[k])
            p.dve(lambda e: e.tensor_copy(out=identb[:], in_=identf[:]), ["identf"], ["identb"])
            p.dve(lambda e: e.memset(onesb[:], 1.0), [], ["onesb"])

            c3 = SB(s0, "c3", [128, DT, 4])
            scT = SB(s0, "scT", [128, DT, 4])
            bmod = SB(s0, "bmod", [128, 48])
            p.dma("sp", c3[:], c3T, writes=["c3"])
            p.dma("sp", bmod[:], bmodT, writes=["bmod"])
            p.act(lambda e: e.activation(out=scT[:], in_=c3[:], func=AF.Silu), ["c3"], ["scT"])
            def ld(name, shape, src):
                t = SB(s0, name, shape)
                p.dma("sp", t[:], src, writes=[name])
                return t
            lre = ld("lre", [128, 32], lamre_d)
            lim = ld("lim", [128, 32], lamim_d)
            ldt = ld("ldt", [128, 32], ldt_d)
            Btre = ld("Btre", [128, 32, 16], Btre_d)
            Btim = ld("Btim", [128, 32, 16], Btim_d)
            Ctre = ld("Ctre", [128, 32, 16], Ctre_d)
            Ctim = ld("Ctim", [128, 32, 16], Ctim_d)
            dcol = ld("dcol", [128, 32], dcol_d)
            kx = ld("kx", [128, 32, 18], kexp_d)
            msk = ld("msk", [128, 2, 128], mask_d)

            def tmp(name, shape, dt=F32):
                return SB(s0, name, shape, dt)

            def TT(o, a, b, op, r, w, eng="dve"):
                p.add(eng, lambda e: e.tensor_tensor(out=o, in0=a, in1=b, op=op), r, w)

            def TS(o, a, s1, s2, op0, op1, r, w, eng="dve"):
                if s2 is None:
                    p.add(eng, lambda e: e.tensor_scalar(out=o, in0=a, scalar1=s1, scalar2=None, op0=op0), r, w)
                else:
                    p.add(eng, lambda e: e.tensor_scalar(out=o, in0=a, scalar1=s1, scalar2=s2, op0=op0, op1=op1), r, w)

            dtv = tmp("dtv", [128, 32])
            a_ = tmp("a_", [128, 32])
            w_ = tmp("w_", [128, 32])
            p.act(lambda e: e.activation(out=dtv[:], in_=ldt[:], func=AF.Exp), ["ldt"], ["dtv"])
            TT(a_[:], lre[:], dtv[:], ALU.mult, ["lre", "dtv"], ["a_"])
            TT(w_[:], lim[:], dtv[:], ALU.mult, ["lim", "dtv"], ["w_"])
            magarg = tmp("magarg", [128, 32, 18])
            angarg = tmp("angarg", [128, 32, 18])
            mag = tmp("mag", [128, 32, 18])
            TT(magarg[:], a_[:].unsqueeze(2).to_broadcast([128, 32, 18]), kx[:], ALU.mult, ["a_", "kx"], ["magarg"])
            TT(angarg[:], w_[:].unsqueeze(2).to_broadcast([128, 32, 18]), kx[:], ALU.mult, ["w_", "kx"], ["angarg"])
            p.act(lambda e: e.activation(out=mag[:], in_=magarg[:], func=AF.Exp), ["magarg"], ["mag"])
            rf = tmp("rf", [128, 32, 18])
            ri_ = tmp("ri_", [128, 32, 18], I32)
            rk = tmp("rk", [128, 32, 18])
            red_s = tmp("red_s", [128, 32, 18])
            red_c = tmp("red_c", [128, 32, 18])
            sinv = tmp("sinv", [128, 32, 18])
            cosv = tmp("cosv", [128, 32, 18])

            def reduce_angle(dst, dk, add):
                TS(rf[:], angarg[:], 1.0 / TWO_PI, add / TWO_PI, ALU.mult, ALU.add, ["angarg"], ["rf"])
                p.dve(lambda e: e.tensor_copy(out=ri_[:], in_=rf[:]), ["rf"], ["ri_"])
                p.dve(lambda e: e.tensor_copy(out=rk[:], in_=ri_[:]), ["ri_"], ["rk"])
                p.dve(lambda e: e.scalar_tensor_tensor(out=dst[:], in0=rk[:], scalar=-TWO_PI, in1=angarg[:],
                                                       op0=ALU.mult, op1=ALU.add), ["rk", "angarg"], [dk])
                TS(dst[:], dst[:], add, None, ALU.add, None, [dk], [dk])
                TS(dst[:], dst[:], -PI_SAFE, PI_SAFE, ALU.max, ALU.min, [dk], [dk])
            reduce_angle(red_s, "red_s", 0.0)
            reduce_angle(red_c, "red_c", math.pi / 2)
            p.act(lambda e: e.activation(out=sinv[:], in_=red_s[:], func=AF.Sin), ["red_s"], ["sinv"])
            p.act(lambda e: e.activation(out=cosv[:], in_=red_c[:], func=AF.Sin), ["red_c"], ["cosv"])
            ApR = tmp("ApR", [128, 32, 18])
            ApI = tmp("ApI", [128, 32, 18])
            TT(ApR[:], mag[:], cosv[:], ALU.mult, ["mag", "cosv"], ["ApR"])
            TT(ApI[:], mag[:], sinv[:], ALU.mult, ["mag", "sinv"], ["ApI"])
            dump("ApR", ApR[:], ["ApR"])
            dump("ApI", ApI[:], ["ApI"])
            for r_ in range(2):
                p.dve(lambda e, r_=r_: e.tensor_copy(out=CRt[:, r_, :], in_=ApR[:, :, 17]), ["ApR"], ["CRt"])
            TS(CIt[:, 0, :], ApI[:, :, 17], -1.0, None, ALU.mult, None, ["ApI"], ["CIt"])
            p.dve(lambda e: e.tensor_copy(out=CIt[:, 1, :], in_=ApI[:, :, 17]), ["ApI"], ["CIt"])
            nr = tmp("nr", [128, 32])
            den = tmp("den", [128, 32])
            t32a = tmp("t32a", [128, 32])
            t32b = tmp("t32b", [128, 32])
            fre = tmp("fre", [128, 32])
            fim = tmp("fim", [128, 32])
            ni = ApI[:, :, 16]
            TS(nr[:], ApR[:, :, 16], -1.0, None, ALU.add, None, ["ApR"], ["nr"])
            TT(t32a[:], lre[:], lre[:], ALU.mult, ["lre"], ["t32a"])
            TT(den[:], lim[:], lim[:], ALU.mult, ["lim"], ["den"])
            TT(den[:], den[:], t32a[:], ALU.add, ["den", "t32a"], ["den"])
            p.dve(lambda e: e.reciprocal(out=den[:], in_=den[:]), ["den"], ["den"])
            TT(t32a[:], nr[:], lre[:], ALU.mult, ["nr", "lre"], ["t32a"])
            TT(t32b[:], ni, lim[:], ALU.mult, ["ApI", "lim"], ["t32b"])
            TT(t32a[:], t32a[:], t32b[:], ALU.add, ["t32a", "t32b"], ["t32a"])
            TT(fre[:], t32a[:], den[:], ALU.mult, ["t32a", "den"], ["fre"])
            TT(t32a[:], ni, lre[:], ALU.mult, ["ApI", "lre"], ["t32a"])
            TT(t32b[:], nr[:], lim[:], ALU.mult, ["nr", "lim"], ["t32b"])
            TT(t32a[:], t32a[:], t32b[:], ALU.subtract, ["t32a", "t32b"], ["t32a"])
            TT(fim[:], t32a[:], den[:], ALU.mult, ["t32a", "den"], ["fim"])
            Bbre = tmp("Bbre", [128, 32, 16])
            Bbim = tmp("Bbim", [128, 32, 16])
            tb1 = tmp("tb1", [128, 32, 16])
            tb2 = tmp("tb2", [128, 32, 16])
            bF = lambda t: t[:].unsqueeze(2).to_broadcast([128, 32, 16])
            TT(tb1[:], bF(fre), Btre[:], ALU.mult, ["fre", "Btre"], ["tb1"])
            TT(tb2[:], bF(fim), Btim[:], ALU.mult, ["fim", "Btim"], ["tb2"])
            TT(Bbre[:], tb1[:], tb2[:], ALU.subtract, ["tb1", "tb2"], ["Bbre"])
            TT(tb1[:], bF(fre), Btim[:], ALU.mult, ["fre", "Btim"], ["tb1"])
            TT(tb2[:], bF(fim), Btre[:], ALU.mult, ["fim", "Btre"], ["tb2"])
            TT(Bbim[:], tb1[:], tb2[:], ALU.add, ["tb1", "tb2"], ["Bbim"])
            dump("fre", fre[:], ["fre"])
            dump("fim", fim[:], ["fim"])
            dump("Bbre", Bbre[:], ["Bbre"])
            dump("Bbim", Bbim[:], ["Bbim"])
            G = tmp("G", [128, 2, 32, 128])
            Hf = tmp("Hf", [128, 2, 32, 128])
            tg1 = tmp("tg1", [128, 16, 8, 16])
            tg2 = tmp("tg2", [128, 16, 8, 16])
            for hv in range(2):
                gs = slice(hv * 16, hv * 16 + 16)

                def pw(tab, lo):
                    return tab[:, gs, lo:lo + 8].unsqueeze(3).to_broadcast([128, 16, 8, 16])

                def vec(t):
                    return t[:, gs, :].unsqueeze(2).to_broadcast([128, 16, 8, 16])

                def v4(t, r_):
                    return t[:, r_, gs, :].rearrange("p g (j q) -> p g j q", q=16)
                TT(tg1[:], pw(ApR, 0), vec(Bbre), ALU.mult, ["ApR", "Bbre"], ["tg1"])
                TT(tg2[:], pw(ApI, 0), vec(Bbim), ALU.mult, ["ApI", "Bbim"], ["tg2"])
                TT(v4(G, 0), tg1[:], tg2[:], ALU.subtract, ["tg1", "tg2"], ["G"])
                TT(tg1[:], pw(ApR, 0), vec(Bbim), ALU.mult, ["ApR", "Bbim"], ["tg1"])
                TT(tg2[:], pw(ApI, 0), vec(Bbre), ALU.mult, ["ApI", "Bbre"], ["tg2"])
                TT(v4(G, 1), tg1[:], tg2[:], ALU.add, ["tg1", "tg2"], ["G"])
                TT(tg1[:], pw(ApR, 8), vec(Ctre), ALU.mult, ["ApR", "Ctre"], ["tg1"])
                TT(tg2[:], pw(ApI, 8), vec(Ctim), ALU.mult, ["ApI", "Ctim"], ["tg2"])
                TT(v4(Hf, 0), tg1[:], tg2[:], ALU.subtract, ["tg1", "tg2"], ["Hf"])
                TT(tg1[:], pw(ApR, 8), vec(Ctim), ALU.mult, ["ApR", "Ctim"], ["tg1"])
                TT(tg2[:], pw(ApI, 8), vec(Ctre), ALU.mult, ["ApI", "Ctre"], ["tg2"])
                p.dve(lambda e, o_=v4(Hf, 1): e.scalar_tensor_tensor(out=o_, in0=tg1[:], scalar=-1.0, in1=tg2[:],
                                                                    op0=ALU.mult, op1=ALU.subtract), ["tg1", "tg2"], ["Hf"])
            dump("G", G[:], ["G"])
            wm = [SB(s0, "wm%d" % i, [128, DT, 256]) for i in range(2)]
            modR = SB(s0, "modR", [4, 6 * D])
            for grp in range(24):
                w_ = wm[grp % 2]
                wk = "wm%d" % (grp % 2)
                bank = 1 + grp % 2
                p.dma("sp", w_[:], w_mod[:, :, grp * 256:(grp + 1) * 256], writes=[wk])
                for dt in range(DT):
                    p.pe(lambda e, w_=w_, dt=dt, bank=bank: e.matmul(
                        PF(bank)[0:4, 0:256], lhsT=scT[:, dt, :], rhs=w_[:, dt, :],
                        start=(dt == 0), stop=(dt == DT - 1)), [wk, "scT"], [PK(bank)])
                p.dve(lambda e, grp=grp, bank=bank: e.tensor_copy(out=modR[:, grp * 256:(grp + 1) * 256], in_=PF(bank)[0:4, 0:256]),
                      [PK(bank)], ["modR"])
            for ft in range(48):
                p.pe(lambda e, ft=ft: e.transpose(out=PF(0)[:, ft * 4:(ft + 1) * 4], in_=modR[0:4, ft * 128:(ft + 1) * 128],
                                                  identity=identf[0:4, 0:4]), ["modR", "identf"], [PK(0)])
            p.dve(lambda e: e.tensor_tensor(
                out=modT[:], in0=PF(0)[:, 0:192].rearrange("p (f r) -> p f r", r=4),
                in1=bmod[:].unsqueeze(2).to_broadcast([128, 48, 4]), op=ALU.add),
                [PK(0), "bmod"], ["modT"])
            dump("modT", modT[:], ["modT"])
            mtmp = SB(s0, "mtmp", [128, DT, 4])
            for (sc, lo, nn, nk) in ((scale1, 8, n1s, "n1s"), (scale2, 32, n2s, "n2s")):
                p.dve(lambda e, lo=lo: e.tensor_scalar(out=mtmp[:], in0=modT[:, lo:lo + 8, :], scalar1=1.0,
                                                       scalar2=None, op0=ALU.add), ["modT"], ["mtmp"])
                p.dve(lambda e, sc=sc, nn=nn: e.tensor_tensor(
                    out=sc[:], in0=mtmp[:], in1=nn[:].unsqueeze(2).to_broadcast([128, DT, 4]), op=ALU.mult),
                    ["mtmp", nk], ["scales"])

            GTw = tmp("GTw", [128, 32, 2, 128], BF16)
            Kw = tmp("Kw", [128, 32, 128], BF16)
            p.dma("pool", hb_scr, Hf[:], reads=["Hf"], writes=["hb_scr"])
            cnt = 0
            for dr in range(2):
                for g0 in range(0, 32, 4):
                    bank = 1 + (cnt % 2)
                    cnt += 1
                    for s_ in range(4):
                        g = g0 + s_
                        gl, gh = g // 16, g % 16
                        gd = dr * 16 + gh
                        for r_ in range(2):
                            p.pe(lambda e, bank=bank, s_=s_, gl=gl, gd=gd, r_=r_: e.matmul(
                                PF(bank)[:, s_ * 128:(s_ + 1) * 128],
                                lhsT=G[gl * 64:(gl + 1) * 64, r_, gd, :], rhs=Hf[gl * 64:(gl + 1) * 64, r_, gd, :],
                                start=(r_ == 0), stop=(r_ == 1)), ["G", "Hf"], [PK(bank)])
                    p.dve(lambda e, bank=bank, dr=dr, g0=g0: e.tensor_tensor(
                        out=Kw[:, g0:g0 + 4, :], in0=PF(bank).rearrange("p (s q) -> p s q", q=128),
                        in1=msk[:, dr, :].unsqueeze(1).to_broadcast([128, 4, 128]), op=ALU.mult),
                        [PK(bank), "msk"], ["Kw"])
                if dr == 0:
                    for g in range(32):
                        p.dve(lambda e, g=g: e.scalar_tensor_tensor(
                            out=Kw[:, g, :], in0=identf[:], scalar=dcol[:, g:g + 1], in1=Kw[:, g, :],
                            op0=ALU.mult, op1=ALU.add), ["identf", "dcol", "Kw"], ["Kw"])
                p.dma("sp", kw_scr[:, dr, :, :], Kw[:], reads=["Kw"], writes=["kw_scr"])
            cnt = 0
            for gd0 in range(0, 32, 2):
                bank = 3 + (cnt % 2)
                cnt += 1
                for s_ in range(4):
                    gd, r_ = gd0 + s_ // 2, s_ % 2
                    p.pe(lambda e, bank=bank, s_=s_, gd=gd, r_=r_: e.transpose(
                        out=PF(bank)[:, s_ * 128:(s_ + 1) * 128], in_=G[:, r_, gd, :], identity=identf[:]),
                        ["G", "identf"], [PK(bank)])
                p.act(lambda e, bank=bank, gd0=gd0: e.activation(
                    out=GTw[:, gd0:gd0 + 2, :, :].rearrange("p a b c -> p (a b c)"), in_=PF(bank), func=AF.Identity),
                    [PK(bank)], ["GTw"])
            p.dma("sp", gt_scr, GTw[:], reads=["GTw"], writes=["gt_scr"])
            p.emit()

        def norm_tile(stk_bufs, src_ap, src_reads, dst_fn, dst_key, scale_t, shift_lo, r_idx, i, defer=None):
            nxs, nxn, nss = len(stk_bufs["xs"]), len(stk_bufs["xn"]), len(stk_bufs["ss"])
            xs, xsk = stk_bufs["xs"][i % nxs], "xs%d" % (i % nxs)
            xn, xnk = stk_bufs["xn"][i % nxn], "xn%d" % (i % nxn)
            ss, ssk = stk_bufs["ss"][i % nss], "ss%d" % (i % nss)
            junk = stk_bufs["junk"]
            if src_reads is None:
                p.dma("sp", xs[:], src_ap, writes=[xsk])
                xin_ap, rk_ = xs[:], [xsk]
            else:
                xin_ap, rk_ = src_ap, list(src_reads)
            p.act(lambda e: e.activation(out=junk[:], in_=xin_ap, func=AF.Square, accum_out=ss[:, 0:1]),
                  rk_, ["junk", ssk])
            p.act(lambda e: e.activation(out=ss[:, 2:3], in_=ss[:, 0:1], func=AF.Sqrt, scale=1.0 / D, bias=EPS),
                  [ssk], [ssk])
            p.dve(lambda e: e.reciprocal(out=ss[:, 3:4], in_=ss[:, 2:3]), [ssk], [ssk])
            p.act(lambda e: e.activation(out=xn[:], in_=xin_ap, func=AF.Identity, scale=ss[:, 3:4]),
                  rk_ + [ssk], [xnk])
            bank = 6 + (i % 2)

            def back():
                for dt in range(DT):
                    p.pe(lambda e, dt=dt: e.transpose(out=PB(bank)[:, dt * 128:(dt + 1) * 128],
                                                      in_=xn[:, dt * 128:(dt + 1) * 128], identity=identb[:]),
                         [xnk, "identb"], [PK(bank)])
                for dt in range(DT):
                    p.dve(lambda e, dt=dt: e.tensor_scalar(
                        out=dst_fn(dt), in0=PB(bank)[:, dt * 128:(dt + 1) * 128],
                        scalar1=scale_t[:, dt, r_idx:r_idx + 1], scalar2=modT[:, shift_lo + dt, r_idx:r_idx + 1],
                        op0=ALU.mult, op1=ALU.add), [PK(bank), "scales", "modT"], [dst_key], big=True)
            if defer is None:
                back()
            else:
                defer.append(back)

        K = 1024
        ARENA_BYTES = 188 * K
        arena = st.enter_context(nc.sbuf_tensor("arena", [128, ARENA_BYTES // 2], BF16))
        R_QM, R_HT, R_AT, R_UT, R_Z, R_KV, R_YS = ((0, 32 * K), (32 * K, 68 * K), (68 * K, 100 * K), (100 * K, 118 * K),
                                                    (118 * K, 154 * K), (154 * K, 172 * K), (172 * K, 188 * K))

        def RG(*ranges):
            return Regs(arena, ranges)
        qm = RG(R_QM).take([128, DT, L], BF16)
        hT = RG(R_HT).take([128, DT, S], BF16)
        attnT = RG(R_AT).take([128, NH, L], BF16)
        UT = RG(R_UT).take([128, 32, NCH], BF16)
        Z = RG(R_Z).take([128, 2, 32, NCH], BF16)
        rkv = RG(R_KV)
        kT = rkv.take([128, 2, S], BF16)
        V = rkv.take([128, NTT, 256], BF16)
        ysT = RG(R_YS).take([128, 4, L], BF16)
        mT = qm
        qT = qm
        h2T = qm

        for b in range(2):
            if True:
                if True:
                    s1 = RG(R_QM, R_KV, R_YS)
                    GTs = SB(s1, "GTs", [128, 32, 2, 128], BF16)
                    Utoks = [SB(s1, "Utok%d" % i, [128, 32, 8, 16], BF16) for i in range(2)]
                    bufs = {"xs": [SB(s1, "xsA%d" % i, [128, D]) for i in range(2)],
                            "xn": [SB(s1, "xnA%d" % i, [128, D], BF16) for i in range(3)],
                            "ss": [SB(s1, "ssA%d" % i, [128, 4]) for i in range(3)],
                            "junk": SB(s1, "junkA", [128, D], BF16)}
                    wu = SB(s1, "wu", [128, DT, 512], BF16)
                    p.dma("pool", wu[:], w_in[:, :, 1536:2048], writes=["wu"])
                    p.dma("sp", GTs[:], gt_scr, reads=["gt_scr"], writes=["GTs"])
                    tcount = [0]

                    nbacks = []

                    def s_norm(sp2, tl):
                        tt = sp2 * 8 + tl
                        r_idx = 2 if tt < 2 else b
                        norm_tile(bufs, xin[b, tt * 128:(tt + 1) * 128, :], None,
                                  lambda dt, tt=tt: hT[:, dt, tt * 128:(tt + 1) * 128], ("hT", tt),
                                  scale1, 0, r_idx, tcount[0], defer=nbacks)
                        tcount[0] += 1
                    for tl in range(8):
                        s_norm(0, tl)
                        if len(nbacks) > 1:
                            nbacks.pop(0)()
                    while nbacks:
                        nbacks.pop(0)()
                    for sp_ in range(3):
                        ntile = 8 if sp_ < 2 else 2
                        ncr = ntile * 16
                        hspan = hT[:, :, sp_ * 1024:sp_ * 1024 + ntile * 128]
                        Utok = Utoks[sp_ % 2]
                        utk = "Utok%d" % (sp_ % 2)
                        hkeys = [("hT", sp_ * 8 + tl) for tl in range(ntile)]
                        nnext = 0 if sp_ == 2 else (8 if sp_ + 1 < 2 else 2)
                        for j in range(8):
                            if j < nnext:
                                s_norm(sp_ + 1, j)
                            bank = j % 2
                            for dt in range(DT):
                                p.pe(lambda e, bank=bank, dt=dt, j=j, ncr=ncr, hspan=hspan: e.matmul(
                                    PF(bank)[0:ncr, :], lhsT=hspan[:, dt, j:j + 8 * (ncr - 1) + 1:8],
                                    rhs=wu[:, dt, :], start=(dt == 0), stop=(dt == DT - 1)),
                                    hkeys + ["wu"], [PK(bank)])
                            p.act(lambda e, bank=bank, j=j, ncr=ncr, Utok=Utok: e.activation(
                                out=Utok[0:ncr, :, j, :], in_=PF(bank)[0:ncr, :].rearrange("p (g q) -> p g q", q=16),
                                func=AF.Identity), [PK(bank)], [utk], big=True)
                            while len(nbacks) > 1:
                                nbacks.pop(0)()
                        while nbacks:
                            nbacks.pop(0)()
                        for g0 in range(0, 32, 8):
                            bank = 2 + (g0 // 8) % 2
                            for s_ in range(8):
                                g = g0 + s_
                                p.pe(lambda e, bank=bank, s_=s_, g=g, ncr=ncr, Utok=Utok: e.transpose(
                                    out=PB(bank)[:, s_ * 128:s_ * 128 + ncr],
                                    in_=Utok[0:ncr, g, :, :].rearrange("p j q -> p (j q)"),
                                    identity=identb[0:ncr, 0:ncr]), [utk, "identb"], [PK(bank)])
                            p.dve(lambda e, bank=bank, g0=g0, ncr=ncr, sp_=sp_: e.tensor_copy(
                                out=UT[:, g0:g0 + 8, sp_ * 128:sp_ * 128 + ncr],
                                in_=PB(bank).rearrange("p (s c) -> p s c", c=128)[:, :, 0:ncr]),
                                [PK(bank)], ["UT"], big=True)
                    if b == 0:
                        dump("UT", UT[:], ["UT"])
                    cnt = 0
                    for gd in range(32):
                        dr, gh = gd // 16, gd % 16
                        for r_ in range(2):
                            bank = 4 + (cnt % 2)
                            cnt += 1
                            for gl in range(2):
                                g = gl * 16 + gh
                                p.pe(lambda e, bank=bank, gd=gd, r_=r_, gl=gl, g=g: e.matmul(
                                    PF(bank)[gl * 64:(gl + 1) * 64, 0:NCH],
                                    lhsT=GTs[:, gd, r_, gl * 64:(gl + 1) * 64], rhs=UT[:, g, :],
                                    start=True, stop=True), ["GTs", "UT"], [PK(bank)])
                            if dr == 0:
                                p.act(lambda e, bank=bank, gd=gd, r_=r_: e.activation(
                                    out=Z[:, r_, gd, :], in_=PF(bank)[:, 0:NCH], func=AF.Identity), [PK(bank)], ["Z"], big=True)
                            else:
                                p.act(lambda e, bank=bank, gd=gd, r_=r_: e.activation(
                                    out=Z[:, r_, gd, 0:32], in_=PF(bank)[:, 31::-1], func=AF.Identity), [PK(bank)], ["Z"], big=True)
                                p.act(lambda e, bank=bank, gd=gd, r_=r_: e.activation(
                                    out=Z[:, r_, gd, 32:NCH], in_=PF(bank)[:, NCH - 1:31:-1], func=AF.Identity),
                                    [PK(bank)], ["Z"], big=True)
                    if b == 0:
                        dump("Z0", Z[:], ["Z"])
                    p.emit()
            if True:
                if True:
                    if True:
                        sQ = RG((R_YS[0], R_YS[0] + 8 * K))
                        s2 = sQ
                        X4 = [SB(s2, "X4_%d" % i, [128, 4, 32]) for i in range(2)]
                        Ts = [SB(s2, "sc_T_%d" % i, [128, 2, 2, 32]) for i in range(2)]
                        NU = 16
                        us = [SB(s2, "sc_u_%d" % i, [128, 2, 32]) for i in range(NU)]
                        pw_pending = []
                        pw_eng = ["act"]
                        PWD = 8
                        p.pool(lambda e: e.memset(X4[0][:], 0.0), [], ["X4_0"])

                        def win(v):
                            return bass.AP(v.tensor, v.offset, [list(v.ap[0]), [32, 2], [32, 2], [1, 32]])

                        def rep2(v):
                            pa = v.ap
                            return bass.AP(v.tensor, v.offset, [list(pa[0]), [0, 2], list(pa[1]), list(pa[2])])

                        def scan_step(s_):
                            xa, xb_ = X4[s_ % 2], X4[(s_ + 1) % 2]
                            ka, kb = "X4_%d" % (s_ % 2), "X4_%d" % ((s_ + 1) % 2)
                            T_, kT_ = Ts[s_ % 2], "sc_T_%d" % (s_ % 2)
                            u_, ku = us[s_ % NU], "sc_u_%d" % (s_ % NU)
                            p.pool(lambda e, xa=xa, T_=T_: e.tensor_tensor(out=T_[:], in0=CRI[:], in1=win(xa), op=ALU.mult),
                                   ["CRt", "CIt", ka], [kT_])
                            p.pool(lambda e, T_=T_, u_=u_: e.tensor_tensor(out=u_[:], in0=T_[:, 0], in1=T_[:, 1], op=ALU.add),
                                   [kT_], [ku])
                            p.pool(lambda e, xb_=xb_, u_=u_, s_=s_: e.tensor_tensor(
                                out=xb_[:].rearrange("p (a b) c -> p a b c", a=2), in0=rep2(u_), in1=rep2(Z[:, :, :, s_]),
                                op=ALU.add), [ku, ("Z", s_)], [kb])
                            if pw_eng[0] == "act":
                                pw_pending.append(lambda u_=u_, s_=s_, ku=ku: p.act(
                                    lambda e: e.activation(out=Z[:, :, :, s_], in_=u_[:], func=AF.Identity), [ku], [("Z", s_)]))
                            else:
                                pw_pending.append(lambda u_=u_, s_=s_, ku=ku: p.dve(
                                    lambda e: e.tensor_copy(out=Z[:, :, :, s_], in_=u_[:]), [ku], [("Z", s_)]))
                            if len(pw_pending) > PWD:
                                pw_pending.pop(0)()
                        NSC_B2 = 84
                        sQ = RG(R_AT, (R_YS[0] + 8 * K, R_YS[1]))
                        wg = [SB(sQ, "wg%d" % i, [128, DT, 512], BF16) for i in range(2)]
                        ropec = SB(sQ, "ropec", [128, 16, 64])
                        ropes = SB(sQ, "ropes", [128, 16, 64])
                        p.dma("sp", ropec[:], ropec_d, writes=["ropec"])
                        p.dma("sp", ropes[:], ropes_d, writes=["ropes"])
                        bufs = {"junk": SB(sQ, "junkB", [128, 128], BF16)}
                        junk4 = SB(sQ, "junk4", [128, 4, 128], BF16)
                        hss = [SB(sQ, "hss%d" % i, [128, 4, 4]) for i in range(3)]
                        for i in range(3):
                            p.dve(lambda e, i=i: e.memset(hss[i][:], 1.0), [], ["hss%d" % i])
                        qn = [SB(sQ, "qn%d" % i, [128, 4, 128]) for i in range(3)]
                        ra = SB(sQ, "ra", [128, 4, 64])
                        rb = SB(sQ, "rb", [128, 4, 64])
                        rc = SB(sQ, "rc", [128, 4, 64])
                        rd_ = SB(sQ, "rd_", [128, 4, 64])
                        qr = [SB(sQ, "qr%d" % i, [128, 4, 128], BF16) for i in range(3)]
                        it = 0
                        sc_next = [0]
                        pending = []
                        for gi, grp in enumerate((2, 0, 1)):
                            w_ = wg[gi % 2]
                            wk = "wg%d" % (gi % 2)
                            if gi == 0:
                                p.dma("pool", wg[0][:], w_in[:, :, 2 * 512:3 * 512], writes=["wg0"])
                                p.dma("pool", wg[1][:], w_in[:, :, 0:512], writes=["wg1"])
                            elif gi == 2:
                                p.dma("pool", w_[:], w_in[:, :, grp * 512:(grp + 1) * 512], writes=[wk])
                            nhd = 4 if grp < 2 else 2
                            nw, nwk = (qnw, "qnw") if grp < 2 else (knw, "knw")
                            for tt in range(2 if grp < 2 else 0, NTT):
                                bank = (0, 1, 4)[it % 3]
                                i2 = it % 3
                                it += 1
                                for _k in range(2):
                                    if sc_next[0] < NSC_B2:
                                        scan_step(sc_next[0])
                                        sc_next[0] += 1
                                for dt in range(DT):
                                    p.pe(lambda e, bank=bank, dt=dt, tt=tt, w_=w_: e.matmul(
                                        PF(bank), lhsT=hT[:, dt, tt * 128:(tt + 1) * 128], rhs=w_[:, dt, :],
                                        start=(dt == 0), stop=(dt == DT - 1)), [("hT", tt), wk], [PK(bank)])
                                hs, hk = hss[i2], "hss%d" % i2
                                q_, qk_ = qn[i2], "qn%d" % i2
                                qr_, qrk = qr[i2], "qr%d" % i2
                                hks = [hk + "_%d" % h_ for h_ in range(nhd)]
                                qks = [qk_ + "_%d" % h_ for h_ in range(nhd)]
                                for h_ in range(nhd):
                                    p.act(lambda e, bank=bank, h_=h_, hs=hs: e.activation(
                                        out=junk4[:, h_, :], in_=PF(bank)[:, h_ * 128:(h_ + 1) * 128],
                                        func=AF.Square, accum_out=hs[:, 0, h_:h_ + 1]), [PK(bank)], ["junk%d" % h_, hks[h_]])
                                p.act(lambda e, hs=hs: e.activation(out=hs[:, 2, :], in_=hs[:, 0, :], func=AF.Sqrt,
                                                                    scale=1.0 / 128, bias=EPS), hks, [hk])
                                p.dve(lambda e, hs=hs: e.reciprocal(out=hs[:, 3, :], in_=hs[:, 2, :]), [hk], [hk])
                                for h_ in range(nhd):
                                    p.dve(lambda e, bank=bank, h_=h_, hs=hs, q_=q_, nw=nw: e.scalar_tensor_tensor(
                                        out=q_[:, h_, :], in0=PF(bank)[:, h_ * 128:(h_ + 1) * 128],
                                        scalar=hs[:, 3, h_:h_ + 1], in1=nw[:], op0=ALU.mult, op1=ALU.mult),
                                        [PK(bank), hk, nwk], [qks[h_]])
                                if grp == 2:
                                    p.act(lambda e, bank=bank, tt=tt: e.activation(
                                        out=V[:, tt, :], in_=PF(bank)[:, 256:512], func=AF.Identity), [PK(bank)], [("V", tt)])
                                if tt >= 2:
                                    m = tt - 2
                                    ev = q_[:, 0:nhd, 0:128:2]
                                    od = q_[:, 0:nhd, 1:128:2]
                                    cs = ropec[:, m, :].unsqueeze(1).to_broadcast([128, nhd, 64])
                                    sn = ropes[:, m, :].unsqueeze(1).to_broadcast([128, nhd, 64])
                                    p.dve(lambda e, ev=ev, cs=cs, nhd=nhd: e.tensor_tensor(out=ra[:, 0:nhd, :], in0=ev, in1=cs, op=ALU.mult),
                                          qks + ["ropec"], ["ra"])
                                    p.dve(lambda e, od=od, sn=sn, nhd=nhd: e.tensor_tensor(out=rb[:, 0:nhd, :], in0=od, in1=sn, op=ALU.mult),
                                          qks + ["ropes"], ["rb"])
                                    p.dve(lambda e, ev=ev, sn=sn, nhd=nhd: e.tensor_tensor(out=rc[:, 0:nhd, :], in0=ev, in1=sn, op=ALU.mult),
                                          qks + ["ropes"], ["rc"])
                                    p.dve(lambda e, od=od, cs=cs, nhd=nhd: e.tensor_tensor(out=rd_[:, 0:nhd, :], in0=od, in1=cs, op=ALU.mult),
                                          qks + ["ropec"], ["rd_"])
                                    p.dve(lambda e, qr_=qr_, nhd=nhd: e.tensor_tensor(out=qr_[:, 0:nhd, 0:128:2], in0=ra[:, 0:nhd, :],
                                                                                      in1=rb[:, 0:nhd, :], op=ALU.subtract),
                                          ["ra", "rb"], [qrk])
                                    p.dve(lambda e, qr_=qr_, nhd=nhd: e.tensor_tensor(out=qr_[:, 0:nhd, 1:128:2], in0=rc[:, 0:nhd, :],
                                                                                      in1=rd_[:, 0:nhd, :], op=ALU.add),
                                          ["rc", "rd_"], [qrk + "o"])
                                else:
                                    p.pool(lambda e, qr_=qr_, q_=q_, nhd=nhd: e.tensor_copy(out=qr_[:, 0:nhd, :], in_=q_[:, 0:nhd, :]),
                                           qks, [qrk])
                                def back(it=it, qr_=qr_, qrk=qrk, nhd=nhd, grp=grp, tt=tt):
                                    tb_ = (2, 3, 5)[it % 3]
                                    for h_ in range(nhd):
                                        p.pe(lambda e, tb_=tb_, h_=h_, qr_=qr_: e.transpose(
                                            out=PB(tb_)[:, h_ * 128:(h_ + 1) * 128], in_=qr_[:, h_, :], identity=identb[:]),
                                            [qrk, qrk + "o", "identb"], [PK(tb_)])
                                    if grp < 2:
                                        m = tt - 2
                                        p.act(lambda e, tb_=tb_, grp=grp, m=m: e.activation(
                                            out=qT[:, grp * 4:(grp + 1) * 4, m * 128:(m + 1) * 128],
                                            in_=PB(tb_)[:, 0:512].rearrange("p (h c) -> p h c", c=128), func=AF.Copy),
                                            [PK(tb_)], [("qT", grp, m // 4)], big=True)
                                    else:
                                        p.act(lambda e, tb_=tb_, tt=tt: e.activation(
                                            out=kT[:, :, tt * 128:(tt + 1) * 128],
                                            in_=PB(tb_)[:, 0:256].rearrange("p (h c) -> p h c", c=128), func=AF.Copy),
                                            [PK(tb_)], [("kT", tt)])
                                pending.append(back)
                                if len(pending) > 1:
                                    pending.pop(0)()
                        while pending:
                            pending.pop(0)()
                        while pw_pending:
                            pw_pending.pop(0)()
                        if b == 0:
                            dump("hT", hT[:], [("hT", i) for i in range(NTT)])
                            dump("qT", qT[:], [("qT", g_, q_) for g_ in range(2) for q_ in range(4)])
                            dump("kT", kT[:], [("kT", i) for i in range(NTT)])
                            dump("V", V[:], [("V", i) for i in range(NTT)])
                        p.emit()
            if True:
                if True:
                    if True:
                        sQ = RG((R_YS[0] + 8 * K, R_YS[1]))
                        pw_eng[0] = "dve"
                        PT = [SB(sQ, "PT%d" % i, [128, 512], BF16) for i in range(4)]
                        rden = [SB(sQ, "rden%d" % i, [128, 512]) for i in range(2)]
                        sc_att = 1.0 / math.sqrt(128.0)
                        LA = 2
                        SB_ = [0, 1, 6, 7]
                        items = [(h_, qb, kt) for h_ in range(NH) for qb in range(4) for kt in range(NTT)]
                        for i in range(len(items) + LA):
                            if (i % 14) in (0, 3, 6, 9, 12) and sc_next[0] < NCH:
                                scan_step(sc_next[0])
                                sc_next[0] += 1
                            if i < len(items):
                                h_, qb, kt = items[i]
                                kvh = h_ // 4
                                bs = SB_[i % 4]
                                pt_, ptk = PT[i % 4], "PT%d" % (i % 4)
                                p.pe(lambda e, bs=bs, kt=kt, kvh=kvh, h_=h_, qb=qb: e.matmul(
                                    PF(bs), lhsT=kT[:, kvh, kt * 128:(kt + 1) * 128],
                                    rhs=qT[:, h_, qb * 512:(qb + 1) * 512], start=True, stop=True),
                                    [("kT", kt), ("qT", h_ // 4, qb)], [PK(bs)])
                                p.act(lambda e, bs=bs, pt_=pt_: e.activation(out=pt_[:], in_=PF(bs), func=AF.Exp,
                                                                             scale=sc_att), [PK(bs)], [ptk])
                            if i >= LA:
                                h_, qb, kt = items[i - LA]
                                kvh = h_ // 4
                                io = (h_ * 4 + qb) % 2
                                bo, bd = 2 + io, 4 + io
                                pt_, ptk = PT[(i - LA) % 4], "PT%d" % ((i - LA) % 4)
                                p.pe(lambda e, bo=bo, kt=kt, kvh=kvh, pt_=pt_: e.matmul(
                                    PF(bo), lhsT=V[:, kt, kvh * 128:(kvh + 1) * 128], rhs=pt_[:],
                                    start=(kt == 0), stop=(kt == NTT - 1)), [("V", kt), ptk], [PK(bo)])
                                p.pe(lambda e, bd=bd, kt=kt, pt_=pt_: e.matmul(
                                    PF(bd), lhsT=onesb[:], rhs=pt_[:],
                                    start=(kt == 0), stop=(kt == NTT - 1)), ["onesb", ptk], [PK(bd)])
                                if kt == NTT - 1:
                                    rd, rdk = rden[io], "rden%d" % io
                                    p.dve(lambda e, bd=bd, rd=rd: e.reciprocal(out=rd[:], in_=PF(bd)), [PK(bd)], [rdk])
                                    p.dve(lambda e, bo=bo, rd=rd, h_=h_, qb=qb: e.tensor_tensor(
                                        out=attnT[:, h_, qb * 512:(qb + 1) * 512], in0=PF(bo), in1=rd[:], op=ALU.mult),
                                        [PK(bo), rdk], [("attnT", qb)], big=True)
                        if b == 0:
                            dump("attnT", attnT[:], [("attnT", i) for i in range(4)])
                        while pw_pending:
                            pw_pending.pop(0)()
                        if b == 0:
                            dump("Z1", Z[:], [("Z", i) for i in range(NCH)])
                        p.emit()
            if True:
                if True:
                    s3 = RG(R_QM)
                    Hs = SB(s3, "Hs", [128, 2, 32, 128], BF16)
                    Ks = SB(s3, "Ks", [128, 2, 32, 128], BF16)
                    YT = RG(R_KV).take([128, 32, 256], BF16)
                    s3 = RG(R_Z, R_UT)
                    Ytok = SB(s3, "Ytok", [128, 2, 8, 512], BF16)
                    gys = [SB(s3, "gy%d" % i, [128, 1024]) for i in range(2)]
                    gts = [SB(s3, "gt_%d" % i, [128, 1024]) for i in range(2)]
                    gss = [SB(s3, "gs_%d" % i, [128, 1024]) for i in range(2)]
                    p.dma("sp", Hs[:], hb_scr, reads=["hb_scr"], writes=["Hs"])
                    p.dma("sp", Ks[:], kw_scr, reads=["kw_scr"], writes=["Ks"])
                    for g in range(32):
                        gl, gh = g // 16, g % 16
                        bank = g % 2
                        first = True
                        for dr in range(2):
                            gd = dr * 16 + gh
                            p.pe(lambda e, bank=bank, dr=dr, g=g, first=first: e.matmul(
                                PF(bank)[:, 0:256], lhsT=Ks[:, dr, g, :], rhs=UT[:, g, 32:NCH],
                                start=first, stop=False), ["Ks", "UT"], [PK(bank)])
                            first = False
                            for r_ in range(2):
                                last = (dr == 1 and r_ == 1)
                                p.pe(lambda e, bank=bank, gl=gl, gd=gd, r_=r_, last=last, dr=dr: e.matmul(
                                    PF(bank)[:, 0:256], lhsT=Hs[gl * 64:(gl + 1) * 64, r_, gd, :],
                                    rhs=(Z[gl * 64:(gl + 1) * 64, r_, gd, 32:NCH] if dr == 0
                                         else Z[gl * 64:(gl + 1) * 64, r_, gd, NCH - 1:31:-1]),
                                    start=False, stop=last), ["Hs", "Z"], [PK(bank)])
                        p.act(lambda e, bank=bank, g=g: e.activation(out=YT[:, g, :], in_=PF(bank)[:, 0:256],
                                                                      func=AF.Identity), [PK(bank)], ["YT"], big=True)
                    if b == 0:
                        dump("YT", YT[:], ["YT"])
                    p.emit()
                    cnt = 0
                    for ct2 in range(2):
                        for g0 in range(0, 32, 8):
                            bank = 2 + (cnt % 2)
                            cnt += 1
                            for s_ in range(8):
                                p.pe(lambda e, bank=bank, s_=s_, g0=g0, ct2=ct2: e.transpose(
                                    out=PB(bank)[:, s_ * 128:(s_ + 1) * 128],
                                    in_=YT[:, g0 + s_, ct2 * 128:(ct2 + 1) * 128], identity=identb[:]),
                                    ["YT", "identb"], [PK(bank)])
                            p.dve(lambda e, bank=bank, g0=g0, ct2=ct2: e.tensor_copy(
                                out=Ytok[:, ct2, :, g0 * 16:(g0 + 8) * 16].rearrange("p t (g q) -> p g t q", q=16),
                                in_=PB(bank).rearrange("p (g t q) -> p g t q", t=8, q=16)), [PK(bank)], ["Ytok"], big=True)
                    cnt = 0
                    for ct2 in range(2):
                        for chq in range(4):
                            bank = 4 + (cnt % 2)
                            cnt += 1
                            for tau in range(8):
                                p.pe(lambda e, bank=bank, tau=tau, ct2=ct2, chq=chq: e.transpose(
                                    out=PB(bank)[:, tau * 128:(tau + 1) * 128],
                                    in_=Ytok[:, ct2, tau, chq * 128:(chq + 1) * 128], identity=identb[:]),
                                    ["Ytok", "identb"], [PK(bank)])
                            gy, gt_, gs_ = gys[cnt % 2], gts[cnt % 2], gss[cnt % 2]
                            kgy, kgt, kgs = "gy%d" % (cnt % 2), "gt_%d" % (cnt % 2), "gs_%d" % (cnt % 2)
                            p.act(lambda e, bank=bank, gy=gy: e.activation(out=gy[:], in_=PB(bank), func=AF.Identity),
                                  [PK(bank)], [kgy])
                            p.dve(lambda e, gy=gy, gt_=gt_: e.tensor_tensor(out=gt_[:], in0=gy[:], in1=gy[:], op=ALU.mult), [kgy], [kgt])
                            p.dve(lambda e, gt_=gt_: e.tensor_scalar(out=gt_[:], in0=gt_[:], scalar1=0.044715, scalar2=1.0,
                                                                     op0=ALU.mult, op1=ALU.add), [kgt], [kgt], big=True)
                            p.dve(lambda e, gy=gy, gt_=gt_: e.tensor_tensor(out=gt_[:], in0=gt_[:], in1=gy[:], op=ALU.mult),
                                  [kgt, kgy], [kgt], big=True)
                            p.act(lambda e, gt_=gt_, gs_=gs_: e.activation(out=gs_[:], in_=gt_[:], func=AF.Sigmoid,
                                                                           scale=2.0 * math.sqrt(2.0 / math.pi)), [kgt], [kgs])
                            p.pool(lambda e, ct2=ct2, chq=chq, gy=gy, gs_=gs_: e.tensor_tensor(
                                out=ysT[:, chq, ct2 * 1024:(ct2 + 1) * 1024].rearrange("p (c t) -> p t c", t=8),
                                in0=gy[:].rearrange("p (t c) -> p t c", c=128),
                                in1=gs_[:].rearrange("p (t c) -> p t c", c=128), op=ALU.mult),
                                [kgy, kgs], ["ysT"], big=True)
                    if b == 0:
                        dump("ysT", ysT[:], ["ysT"])
                    p.emit()
            if True:
                if True:
                    if True:
                        sG = RG(R_UT, R_Z, R_KV)
                        wab = [SB(sG, "wab%d" % i, [128, DT, 128], BF16) for i in range(3)]
                        wga = [SB(sG, "wga%d" % i, [128, DT, 128], BF16) for i in range(3)]
                        wgs = [SB(sG, "wgs%d" % i, [128, DT, 128], BF16) for i in range(3)]
                        wla = [SB(sG, "wla%d" % i, [128, 4, 128], BF16) for i in range(3)]
                        wlb = [SB(sG, "wlb%d" % i, [128, 4, 128], BF16) for i in range(3)]

                        def issue_mw(ft_):
                            j_ = ft_ % 3
                            p.dma("pool", wab[j_][:], w_ab[:, ft_, :, :], writes=["wab%d" % j_])
                            p.dma("pool", wga[j_][:], w_ing[:, ft_, :, :], writes=["wga%d" % j_])
                            p.dma("pool", wgs[j_][:], w_ing[:, 8 + ft_, :, :], writes=["wgs%d" % j_])
                            p.dma("pool", wla[j_][:], w_glu[:, ft_, :, :], writes=["wla%d" % j_])
                            p.dma("pool", wlb[j_][:], w_glu[:, 8 + ft_, :, :], writes=["wlb%d" % j_])
                        issue_mw(0)
                        issue_mw(1)
                        sga = SB(sG, "sga", [128, 512])
                        sgs = SB(sG, "sgs", [128, 512])
                        sgb = SB(sG, "sgb", [128, 512])
                        m1 = SB(sG, "m1", [128, 512])
                        m2 = SB(sG, "m2", [128, 512])
                        bk = 0
                        for ft in range(DT):
                            i2 = ft % 3
                            ks = ["wab%d" % i2, "wga%d" % i2, "wgs%d" % i2, "wla%d" % i2, "wlb%d" % i2]
                            if ft + 2 < DT:
                                issue_mw(ft + 2)
                            for tb in range(4):
                                tok = slice(tb * 512, (tb + 1) * 512)
                                htok = slice(LC + tb * 512, LC + (tb + 1) * 512)
                                hkeys = [("hT", 2 + tb * 4 + i) for i in range(4)]
                                banks = [(bk + i) % 8 for i in range(5)]
                                bk += 5
                                b_pa, b_ga, b_gs, b_ua, b_ub = banks
                                for dt in range(DT):
                                    p.pe(lambda e, dt=dt, b_=b_pa, tok=tok, w=wab[i2]: e.matmul(
                                        PF(b_), lhsT=w[:, dt, :], rhs=attnT[:, dt, tok], start=(dt == 0), stop=(dt == DT - 1)),
                                        [ks[0], ("attnT", tb)], [PK(b_pa)])
                                for dt in range(DT):
                                    p.pe(lambda e, dt=dt, b_=b_ga, htok=htok, w=wga[i2]: e.matmul(
                                        PF(b_), lhsT=w[:, dt, :], rhs=hT[:, dt, htok], start=(dt == 0), stop=(dt == DT - 1)),
                                        [ks[1]] + hkeys, [PK(b_ga)])
                                for dt in range(DT):
                                    p.pe(lambda e, dt=dt, b_=b_gs, htok=htok, w=wgs[i2]: e.matmul(
                                        PF(b_), lhsT=w[:, dt, :], rhs=hT[:, dt, htok], start=(dt == 0), stop=(dt == DT - 1)),
                                        [ks[2]] + hkeys, [PK(b_gs)])
                                for k4 in range(4):
                                    p.pe(lambda e, k4=k4, b_=b_ua, tok=tok, w=wla[i2]: e.matmul(
                                        PF(b_), lhsT=w[:, k4, :], rhs=ysT[:, k4, tok], start=(k4 == 0), stop=(k4 == 3)),
                                        [ks[3], "ysT"], [PK(b_ua)])
                                for k4 in range(4):
                                    p.pe(lambda e, k4=k4, b_=b_ub, tok=tok, w=wlb[i2]: e.matmul(
                                        PF(b_), lhsT=w[:, k4, :], rhs=ysT[:, k4, tok], start=(k4 == 0), stop=(k4 == 3)),
                                        [ks[4], "ysT"], [PK(b_ub)])
                                p.act(lambda e, b_=b_ga: e.activation(out=sga[:], in_=PF(b_), func=AF.Sigmoid), [PK(b_ga)], ["sga"])
                                p.act(lambda e, b_=b_gs: e.activation(out=sgs[:], in_=PF(b_), func=AF.Sigmoid), [PK(b_gs)], ["sgs"])
                                p.act(lambda e, b_=b_ub: e.activation(out=sgb[:], in_=PF(b_), func=AF.Sigmoid), [PK(b_ub)], ["sgb"])
                                p.dve(lambda e, b_=b_pa: e.tensor_tensor(out=m1[:], in0=PF(b_), in1=sga[:], op=ALU.mult),
                                      [PK(b_pa), "sga"], ["m1"])
                                p.dve(lambda e, b_=b_ua: e.tensor_tensor(out=m2[:], in0=PF(b_), in1=sgb[:], op=ALU.mult),
                                      [PK(b_ua), "sgb"], ["m2"])
                                p.pool(lambda e: e.tensor_tensor(out=m2[:], in0=m2[:], in1=sgs[:], op=ALU.mult),
                                       ["m2", "sgs"], ["m2"])
                                p.pool(lambda e, ft=ft, tok=tok: e.tensor_tensor(out=mT[:, ft, tok], in0=m1[:], in1=m2[:], op=ALU.add),
                                       ["m1", "m2"], [("mT", tb)])
                        if b == 0:
                            dump("mT", mT[:], [("mT", i) for i in range(4)])
                        p.emit()
            if True:
                if True:
                    sO = RG(R_HT, R_AT, R_UT, R_Z)
                    xm = [SB(sO, "xm%d" % i, [128, 4, D]) for i in range(2)]
                    xg = [SB(sO, "xg%d" % i, [128, 512]) for i in range(2)]
                    wo = SB(sO, "wo", [128, DT, D], BF16)
                    xm.append(SB(sO, "xm2", [128, 4, D]))

                    def load_xm(tb_):
                        p.dma("sp", xm[tb_ % 3][:],
                              xin[b, LC + tb_ * 512:LC + (tb_ + 1) * 512, :].rearrange("(t p) d -> p t d", p=128),
                              writes=["xm%d" % (tb_ % 3)])
                    load_xm(0)
                    bufs2 = {"xs": [None, None],
                             "xn": [SB(sO, "xnD%d" % i, [128, D], BF16) for i in range(3)],
                             "ss": [SB(sO, "ssD%d" % i, [128, 4]) for i in range(3)],
                             "junk": SB(sO, "junkD", [128, D], BF16)}
                    g1b = SB(sO, "g1b", [128, D])
                    gTs = SB(sO, "gTs", [8, 128])
                    Esel = SB(sO, "Esel", [8, 8, 128])
                    p.dma("pool", wo[:], w_out, writes=["wo"])
                    p.pe(lambda e: e.transpose(out=PF(2)[0:8, 0:128], in_=modT[:, 16:24, b], identity=identf[:]),
                         ["modT", "identf"], [PK(2)])
                    p.act(lambda e: e.activation(out=gTs[:], in_=PF(2)[0:8, 0:128], func=AF.Identity), [PK(2)], ["gTs"])
                    for ft in range(DT):
                        p.dve(lambda e, ft=ft: e.tensor_copy(out=Esel[:, ft, :], in_=identf[0:8, ft:ft + 1].to_broadcast([8, 128])),
                              ["identf"], ["Esel"])
                    for ft in range(DT):
                        bk_ = 4 + ft // 4
                        p.pe(lambda e, ft=ft, bk_=bk_: e.matmul(PF(bk_)[:, (ft % 4) * 128:(ft % 4 + 1) * 128], lhsT=Esel[:, ft, :],
                                                                 rhs=gTs[:], start=True, stop=True), ["Esel", "gTs"], [PK(bk_)])
                    for hh in range(2):
                        p.act(lambda e, hh=hh: e.activation(out=g1b[:, hh * 512:(hh + 1) * 512], in_=PF(4 + hh), func=AF.Identity),
                              [PK(4 + hh)], ["g1b"])
                    ig = 0
                    norm_pending = []
                    n2backs = []
                    for tb in range(4):
                        xm_, xmk = xm[tb % 3], "xm%d" % (tb % 3)
                        if tb + 1 < 4:
                            load_xm(tb + 1)
                        for t4 in range(4):
                            while len(n2backs) > 1:
                                n2backs.pop(0)()
                            if norm_pending:
                                norm_pending.pop(0)()
                            tt_ = tb * 4 + t4
                            for hh in range(2):
                                bank = ig % 2
                                xg_, xgk = xg[ig % 2], "xg%d" % (ig % 2)
                                ig += 1
                                for dt in range(DT):
                                    p.pe(lambda e, bank=bank, dt=dt, tt_=tt_, hh=hh: e.matmul(
                                        PF(bank), lhsT=mT[:, dt, tt_ * 128:(tt_ + 1) * 128], rhs=wo[:, dt, hh * 512:(hh + 1) * 512],
                                        start=(dt == 0), stop=(dt == DT - 1)), ["wo", ("mT", tb)], [PK(bank)])
                                p.dve(lambda e, bank=bank, xg_=xg_, hh=hh: e.tensor_tensor(
                                    out=xg_[:], in0=PF(bank), in1=g1b[:, hh * 512:(hh + 1) * 512], op=ALU.mult),
                                    [PK(bank), "g1b"], [xgk])
                                p.pool(lambda e, xg_=xg_, xm_=xm_, t4=t4, hh=hh: e.tensor_tensor(
                                    out=xm_[:, t4, hh * 512:(hh + 1) * 512], in0=xg_[:], in1=xm_[:, t4, hh * 512:(hh + 1) * 512],
                                    op=ALU.add), [xgk, xmk], [xmk], big=True)
                        p.dma("sp", out[b, tb * 512:(tb + 1) * 512, :].rearrange("(t p) d -> p t d", p=128), xm_[:],
                              reads=[xmk], writes=["outscr"])
                        if b == 0 and debug:
                            p.dma("sp", dbg_xmid[tb * 512:(tb + 1) * 512, :].rearrange("(t p) d -> p t d", p=128), xm_[:],
                                  reads=[xmk], writes=["dbg_xmid"])

                        for t4 in range(4):
                            def norm2(tb=tb, xm_=xm_, xmk=xmk, t4=t4):
                                tt_ = tb * 4 + t4
                                norm_tile(bufs2, xm_[:, t4, :], [xmk],
                                          lambda dt, tt_=tt_: qm[:, dt, tt_ * 128:(tt_ + 1) * 128], ("mT", tb),
                                          scale2, 24, b, tt_, defer=n2backs)
                            norm_pending.append(norm2)
                    while norm_pending or n2backs:
                        while n2backs:
                            n2backs.pop(0)()
                        if norm_pending:
                            norm_pending.pop(0)()
                    p.emit()
            if True:
                if True:
                    sF = RG((32 * K, 188 * K))
                    fnw = SB(sF, "fnw", [128, D])
                    p.dma("sp", fnw[:], fnw_d, writes=["fnw"])
                    aT = SB(sF, "aT", [128, NFT, 1024], BF16)
                    NWU = 6
                    wup = [SB(sF, "wup%d" % i, [128, DT, 128], BF16) for i in range(NWU)]
                    wdn = [SB(sF, "wdn%d" % i, [128, NFT, 128], BF16) for i in range(3)]
                    zb = [SB(sF, "zb%d" % i, [128, 1026]) for i in range(2)]
                    cgs = [SB(sF, "cg%d" % i, [128, 1024]) for i in range(4)]
                    sgs_ = [SB(sF, "sg_%d" % i, [128, 1024], BF16) for i in range(2)]
                    cg = cgs[0]
                    yg = [SB(sF, "yg%d" % i, [128, 512]) for i in range(2)]
                    xo = [SB(sF, "xo%d" % i, [128, 4, D]) for i in range(2)]
                    ot = [SB(sF, "ot%d" % i, [128, D]) for i in range(2)]
                    fss = [SB(sF, "fss%d" % i, [128, 4]) for i in range(2)]
                    iw = 0
                    iz = 0
                    for hf in range(2):
                        t0 = hf * 1024
                        tiles = [(ft, which) for ft in range(NFT) for which in range(2)]

                        def issue_wup(idx, iw0):
                            ft_, wh_ = tiles[idx]
                            fc_ = wh_ * NFT + ft_
                            k_ = (iw0 + idx) % NWU
                            p.dma("pool", wup[k_][:], w_up[:, fc_, :, :], writes=["wup%d" % k_])
                        iw0 = iw
                        ffn_pending = []
                        for zi in range(2):
                            zc_ = 0 if hf == 0 else 1025
                            p.pool(lambda e, zi=zi, zc_=zc_: e.memset(zb[zi][:, zc_:zc_ + 1], 0.0), [], ["zb%d" % zi])
                        if hf == 0:
                            for idx in range(4):
                                issue_wup(idx, iw0)
                        for ft in range(NFT):
                            if ft == NFT - 6:
                                for m2_ in range(2):
                                    p.dma("pool", wdn[m2_][:], w_down[:, m2_, :, :], writes=["wdn%d" % m2_])
                                for t2 in range(2):
                                    p.dma("sp", xo[t2][:],
                                          out[b, t0 + t2 * 512:t0 + (t2 + 1) * 512, :].rearrange("(t p) d -> p t d", p=128),
                                          reads=["outscr"], writes=["xo%d" % t2])
                            for which in range(2):
                                fcol = which * NFT + ft
                                w_, wk = wup[iw % NWU], "wup%d" % (iw % NWU)
                                if iw - iw0 + 4 < len(tiles):
                                    issue_wup(iw - iw0 + 4, iw0)
                                iw += 1
                                z_, zk = zb[iz % 2], "zb%d" % (iz % 2)
                                iz += 1
                                b0_ = (iz % 2) * 3
                                for t2 in range(2):
                                    bank = b0_ + t2
                                    for dt in range(DT):
                                        p.pe(lambda e, bank=bank, dt=dt, w_=w_, t2=t2, t0=t0: e.matmul(
                                            PF(bank), lhsT=w_[:, dt, :], rhs=h2T[:, dt, t0 + t2 * 512:t0 + (t2 + 1) * 512],
                                            start=(dt == 0), stop=(dt == DT - 1)),
                                            [wk, ("h2T", hf * 2 + t2)], [PK(bank)])
                                    p.act(lambda e, bank=bank, z_=z_, t2=t2: e.activation(
                                        out=z_[:, 1 + t2 * 512:1 + (t2 + 1) * 512], in_=PF(bank), func=AF.Identity),
                                        [PK(bank)], [zk], big=True)
                                hb_ = b0_ + 2
                                hcol = (t0 + 1024) if hf == 0 else (t0 - 1)
                                for dt in range(DT):
                                    p.pe(lambda e, hb_=hb_, dt=dt, w_=w_, hcol=hcol: e.matmul(
                                        PF(hb_)[:, 0:1], lhsT=w_[:, dt, :], rhs=h2T[:, dt, hcol:hcol + 1],
                                        start=(dt == 0), stop=(dt == DT - 1)),
                                        [wk, ("h2T", (hcol // 512))], [PK(hb_)])
                                if hf == 0:
                                    p.act(lambda e, hb_=hb_, z_=z_: e.activation(out=z_[:, 1025:1026], in_=PF(hb_)[:, 0:1],
                                                                                 func=AF.Identity), [PK(hb_)], [zk], big=True)
                                else:
                                    p.act(lambda e, hb_=hb_, z_=z_: e.activation(out=z_[:, 0:1], in_=PF(hb_)[:, 0:1],
                                                                                 func=AF.Identity), [PK(hb_)], [zk], big=True)
                                cw = lambda j, fcol=fcol: convT[:, j, fcol:fcol + 1]
                                cg, cgk = cgs[iz % 4], "cg%d" % (iz % 4)
                                sg_, sgk = sgs_[(iz // 2) % 2], "sg_%d" % ((iz // 2) % 2)
                                p.act(lambda e, z_=z_, cw=cw, cg=cg: e.activation(out=cg[:], in_=z_[:, 1:1025], func=AF.Identity,
                                                                                  scale=cw(1), bias=cw(3)), [zk, "convT"], [cgk], big=True)
                                p.dve(lambda e, z_=z_, cw=cw, cg=cg: e.scalar_tensor_tensor(out=cg[:], in0=z_[:, 0:1024], scalar=cw(0),
                                                                                            in1=cg[:], op0=ALU.mult, op1=ALU.add),
                                      [zk, "convT", cgk], [cgk])
                                p.dve(lambda e, z_=z_, cw=cw, cg=cg: e.scalar_tensor_tensor(
                                    out=cg[:], in0=z_[:, 2:1026], scalar=cw(2), in1=cg[:], op0=ALU.mult, op1=ALU.add),
                                    [zk, "convT", cgk], [cgk], big=True)
                                if which == 0:
                                    cg_val, cgk_val = cg, cgk
                                else:
                                    def gate_tail(cg=cg, sg_=sg_, cgk=cgk, sgk=sgk, ft=ft, cg_val=cg_val, cgk_val=cgk_val):
                                        p.act(lambda e, cg=cg, sg_=sg_: e.activation(out=sg_[:], in_=cg[:], func=AF.Silu), [cgk], [sgk])
                                        p.pool(lambda e, ft=ft, sg_=sg_, cg_val=cg_val: e.tensor_tensor(
                                            out=aT[:, ft, :], in0=cg_val[:], in1=sg_[:], op=ALU.mult),
                                            [cgk_val, sgk], [("aT", ft)])
                                    ffn_pending.append(gate_tail)
                                if which == 0 and ffn_pending:
                                    ffn_pending.pop(0)()
                        while ffn_pending:
                            ffn_pending.pop(0)()
                        if hf == 0:
                            for idx in range(4):
                                issue_wup(idx, iw)
                        ig = 0
                        dn_pending = []
                        for mt in range(DT):
                            wd_, wdk = wdn[mt % 3], "wdn%d" % (mt % 3)
                            if mt + 2 < DT:
                                p.dma("pool", wdn[(mt + 2) % 3][:], w_down[:, mt + 2, :, :], writes=["wdn%d" % ((mt + 2) % 3)])
                            for t2 in range(2):
                                bank = 6 + ig % 2
                                tbk = 2 + 3 * (ig % 2)
                                yg_, ygk = yg[ig % 2], "yg%d" % (ig % 2)
                                ig += 1
                                for k in range(NFT):
                                    p.pe(lambda e, bank=bank, k=k, wd_=wd_, t2=t2: e.matmul(
                                        PF(bank), lhsT=wd_[:, k, :], rhs=aT[:, k, t2 * 512:(t2 + 1) * 512],
                                        start=(k == 0), stop=(k == NFT - 1)), [wdk, ("aT", k)], [PK(bank)])
                                p.act(lambda e, bank=bank, yg_=yg_, mt=mt: e.activation(
                                    out=yg_[:], in_=PF(bank), func=AF.Identity, scale=modT[:, 40 + mt, b:b + 1]),
                                    [PK(bank), "modT"], [ygk])
                                def dn_tail(tbk=tbk, yg_=yg_, ygk=ygk, t2=t2, mt=mt):
                                    for t4 in range(4):
                                        p.pe(lambda e, tbk=tbk, t4=t4, yg_=yg_: e.transpose(
                                            out=PF(tbk)[:, t4 * 128:(t4 + 1) * 128], in_=yg_[:, t4 * 128:(t4 + 1) * 128],
                                            identity=identf[:]), [ygk, "identf"], [PK(tbk)])
                                    xo_, xok = xo[t2], "xo%d" % t2
                                    p.dve(lambda e, tbk=tbk, xo_=xo_, mt=mt: e.tensor_tensor(
                                        out=xo_[:, :, mt * 128:(mt + 1) * 128], in0=PF(tbk).rearrange("p (t c) -> p t c", c=128),
                                        in1=xo_[:, :, mt * 128:(mt + 1) * 128], op=ALU.add), [PK(tbk), xok], [xok])
                                dn_pending.append(dn_tail)
                                if len(dn_pending) > 1:
                                    dn_pending.pop(0)()
                        while dn_pending:
                            dn_pending.pop(0)()
                        io_ = 0
                        for t2 in range(2):
                            xo_, xok = xo[t2], "xo%d" % t2
                            for t4 in range(4):
                                fs, fsk = fss[io_ % 2], "fss%d" % (io_ % 2)
                                o_, ok_ = ot[io_ % 2], "ot%d" % (io_ % 2)
                                io_ += 1
                                p.act(lambda e, xo_=xo_, t4=t4, fs=fs: e.activation(
                                    out=cgs[0][:], in_=xo_[:, t4, :], func=AF.Square, accum_out=fs[:, 0:1]),
                                    [xok], ["cg0", fsk])
                                p.act(lambda e, fs=fs: e.activation(out=fs[:, 2:3], in_=fs[:, 0:1], func=AF.Sqrt,
                                                                    scale=1.0 / D, bias=EPS), [fsk], [fsk])
                                p.dve(lambda e, fs=fs: e.reciprocal(out=fs[:, 3:4], in_=fs[:, 2:3]), [fsk], [fsk])
                                p.dve(lambda e, xo_=xo_, t4=t4, fs=fs, o_=o_: e.scalar_tensor_tensor(
                                    out=o_[:], in0=xo_[:, t4, :], scalar=fs[:, 3:4], in1=fnw[:], op0=ALU.mult, op1=ALU.mult),
                                    [xok, fsk, "fnw"], [ok_])
                                r0 = t0 + t2 * 512 + t4 * 128
                                p.dma("sp", out[b, r0:r0 + 128, :], o_[:], reads=[ok_], writes=["outfin"])
                    p.emit()
    return nc


def _pm(w, kt):
    w = np.asarray(w, dtype=np.float32)
    return np.ascontiguousarray(w.reshape(kt, 128, w.shape[1]).transpose(1, 0, 2))


def _pmt(w, kt):
    w = np.asarray(w, dtype=np.float32)
    nt = w.shape[1] // 128
    return np.ascontiguousarray(w.reshape(kt, 128, nt, 128).transpose(1, 2, 0, 3))


def _glT(a):
    a = np.asarray(a, dtype=np.float32)
    rest = a.shape[3:]
    a = a.reshape((2, 2, 16, 64) + rest)
    a = np.moveaxis(a, (1, 3, 0, 2), (0, 1, 2, 3))
    return np.ascontiguousarray(a.reshape((128, 32) + rest))


_NC_CACHE = {}


def kernel(x, c, ctx, c_ctx, w_mod, b_mod, norm1_w, norm2_w, w_in, q_norm_w, k_norm_w,
           w_attn_br, ssm_lambda_re, ssm_lambda_im, ssm_log_dt, ssm_b_re, ssm_b_im,
           ssm_c_re, ssm_c_im, ssm_d, w_glu, w_out, w_up, conv_w, conv_b, w_down,
           final_norm_w):
    f32 = np.float32
    x = np.asarray(x, f32)
    ctx = np.asarray(ctx, f32)
    c = np.asarray(c, f32)
    c_ctx = np.asarray(c_ctx, f32)
    n_cores = 8
    ident = np.eye(128, dtype=f32)
    inv_freq = (10000.0 ** (-np.arange(0, 64, 2, dtype=np.float32) / 64.0)).astype(f32)
    t = np.arange(L)
    rows = (t // 64).astype(f32)
    cols = (t % 64).astype(f32)
    ang = np.concatenate([rows[:, None] * inv_freq, cols[:, None] * inv_freq], axis=-1).astype(f32)
    ropec = np.ascontiguousarray(np.cos(ang).astype(f32).reshape(16, 128, 64).transpose(1, 0, 2))
    ropes = np.ascontiguousarray(np.sin(ang).astype(f32).reshape(16, 128, 64).transpose(1, 0, 2))
    kexp = np.zeros((128, 32, 18), f32)
    j8 = np.arange(8, dtype=f32)
    kexp[:, 0:16, 0:8] = -j8
    kexp[:, 0:16, 8:16] = j8
    kexp[:, 16:32, 0:8] = j8 - 7.0
    kexp[:, 16:32, 8:16] = 7.0 - j8
    kexp[:, :, 16] = 1.0
    kexp[:, :, 17] = 8.0
    jj = np.repeat(np.arange(8), 16)
    mask = np.zeros((128, 2, 128), f32)
    mask[:, 0, :] = (jj[None, :] >= jj[:, None])
    mask[:, 1, :] = (jj[None, :] <= jj[:, None])

    shared = {
        "w_mod": _pm(w_mod[0], 8),
        "bmodT": np.ascontiguousarray(np.asarray(b_mod[0], f32).reshape(48, 128).T),
        "n1T": np.ascontiguousarray(np.asarray(norm1_w[0], f32).reshape(8, 128).T),
        "n2T": np.ascontiguousarray(np.asarray(norm2_w[0], f32).reshape(8, 128).T),
        "fnw_b": np.ascontiguousarray(np.broadcast_to(np.asarray(final_norm_w, f32)[None, :], (128, D))),
        "w_in": _pm(w_in[0], 8),
        "qnw_b": np.ascontiguousarray(np.broadcast_to(np.asarray(q_norm_w[0], f32)[None, :], (128, 128))),
        "knw_b": np.ascontiguousarray(np.broadcast_to(np.asarray(k_norm_w[0], f32)[None, :], (128, 128))),
        "w_ab": _pmt(w_attn_br[0], 8),
        "w_ing": _pmt(np.asarray(w_in[0], np.float32)[:, 2048:], 8),
        "w_glu": _pmt(w_glu[0], 4),
        "w_out": _pm(w_out[0], 8),
        "w_up": _pmt(w_up[0], 8),
        "w_down": _pmt(w_down[0], NFT),
        "convT": np.ascontiguousarray(np.concatenate(
            [np.asarray(conv_w[0], f32), np.asarray(conv_b[0], f32)[None, :]], axis=0
        ).reshape(4, 2 * NFT, 128).transpose(2, 0, 1)),
        "lamT_re": _glT(ssm_lambda_re[0]),
        "lamT_im": _glT(ssm_lambda_im[0]),
        "ldtT": _glT(np.broadcast_to(np.asarray(ssm_log_dt[0], f32)[:, :, None], (2, 32, 64))),
        "Bt_re": _glT(ssm_b_re[0]),
        "Bt_im": _glT(ssm_b_im[0]),
        "Ct_re": _glT(np.asarray(ssm_c_re[0], f32).transpose(0, 1, 3, 2)),
        "Ct_im": _glT(np.asarray(ssm_c_im[0], f32).transpose(0, 1, 3, 2)),
        "dcol": np.ascontiguousarray(np.broadcast_to(
            np.asarray(ssm_d[0], f32).reshape(32, 16).T[None, :, :], (8, 16, 32)).reshape(128, 32)),
        "ident": ident, "ropec": ropec, "ropes": ropes, "kexp": kexp, "mask": mask,
    }
    in_maps = []
    for i in range(n_cores):
        bs = slice(2 * i, 2 * i + 2)
        xin = np.ascontiguousarray(np.concatenate([ctx[bs], x[bs]], axis=1))
        c3 = np.concatenate([c[bs], c_ctx[None, :], np.zeros((1, D), f32)], axis=0)
        c3T = np.ascontiguousarray(c3.reshape(4, 8, 128).transpose(2, 1, 0))
        m = dict(shared)
        m["xin"] = xin
        m["c3T"] = c3T
        in_maps.append(m)
    if _NC_CACHE.get("debug"):
        nc = build_nc(debug=True)
        res = run_bass_kernel_spmd(nc, in_maps[:1], core_ids=[0])
        _NC_CACHE["dbg_res"] = res.results[0]
        _NC_CACHE["dbg_in"] = in_maps[0]
        return None
    if "nc" not in _NC_CACHE:
        _NC_CACHE["nc"] = build_nc()
    nc = _NC_CACHE["nc"]
    res = run_bass_kernel_spmd(nc, in_maps, core_ids=list(range(n_cores)))
    outs = [np.asarray(r["out"], dtype=f32) for r in res.results]
    return np.concatenate(outs, axis=0)
```

```python
import contextlib
import math
import numpy as np
import concourse.bass as bass
import concourse.mybir as mybir
from concourse.bass_utils import run_bass_kernel_spmd

F32 = mybir.dt.float32
BF16 = mybir.dt.bfloat16
I32 = mybir.dt.int32
AF = mybir.ActivationFunctionType
ALU = mybir.AluOpType

D = 1024
DT = 8
L = 2048
LC = 256
S = L + LC
NTT = S // 128
NCH = S // 8
NH = 8
DFF = 2816
NFT = DFF // 128
EPS = 1e-6
TWO_PI = 2.0 * math.pi
PI_SAFE = 3.1415925
ENGS = ("pe", "act", "dve", "pool", "sp")


class Op:
    __slots__ = ("eng", "fn", "idx", "deps", "same", "is_dma", "inc", "sem", "val", "waits")

    def __init__(self, eng, fn, idx, is_dma):
        self.eng = eng
        self.fn = fn
        self.idx = idx
        self.deps = set()
        self.same = set()
        self.is_dma = is_dma
        self.inc = False
        self.sem = None
        self.val = 0
        self.waits = []


class Prog:
    N_DMA_SEMS = 24

    def __init__(self, nc, st):
        self.nc = nc
        self.st = st
        self.eng_sem = {e: st.enter_context(nc.semaphore("s_" + e)) for e in ENGS}
        self.dma_sems = [st.enter_context(nc.semaphore("s_dma%d" % i)) for i in range(self.N_DMA_SEMS)]
        self.dma_count = [0] * self.N_DMA_SEMS
        self.dma_rr = 0
        self.cnt = {e: 0 for e in ENGS}
        self.nsem = 0
        self._reset()

    def _reset(self):
        self.eng_ops = {e: [] for e in ENGS}
        self.last_writer = {}
        self.readers = {}
        self.all_ops = []

    def add(self, eng, fn, reads=(), writes=(), dma=False, big=False):
        op = Op(eng, fn, len(self.eng_ops[eng]), dma)
        for r in reads:
            w = self.last_writer.get(r)
            if w is not None:
                op.deps.add(w)
                if w.eng == eng and eng != "pe":
                    op.same.add(w)
        for w in writes:
            lw = self.last_writer.get(w)
            if lw is not None:
                op.deps.add(lw)
                if lw.eng == eng and eng != "pe":
                    op.same.add(lw)
            for rd in self.readers.get(w, ()):
                op.deps.add(rd)
        for r in reads:
            self.readers.setdefault(r, []).append(op)
        for w in writes:
            self.last_writer[w] = op
            self.readers[w] = []
        op.deps.discard(op)
        if big:
            op.same.clear()
        self.eng_ops[eng].append(op)
        self.all_ops.append(op)
        return op

    def pe(self, fn, reads=(), writes=()):
        return self.add("pe", fn, reads, writes)

    def act(self, fn, reads=(), writes=(), big=False):
        return self.add("act", fn, reads, writes, big=big)

    def dve(self, fn, reads=(), writes=(), big=False):
        return self.add("dve", fn, reads, writes, big=big)

    def pool(self, fn, reads=(), writes=(), big=False):
        return self.add("pool", fn, reads, writes, big=big)

    def dma(self, eng, out, in_, reads=(), writes=()):
        return self.add(eng, lambda e: e.dma_start(out=out, in_=in_), reads, writes, dma=True)

    def emit(self):
        nc = self.nc
        for op in self.all_ops:
            if op.is_dma:
                op.inc = True
            for d in op.deps:
                if d.is_dma or d.eng != op.eng or d in op.same:
                    d.inc = True
        dma_last = [None] * self.N_DMA_SEMS
        for op in self.all_ops:
            if not op.inc:
                continue
            if op.is_dma:
                k = self.dma_rr % self.N_DMA_SEMS
                self.dma_rr += 1
                prev = dma_last[k]
                if prev is not None:
                    op.deps.add(prev)
                self.dma_count[k] += 16
                op.sem = self.dma_sems[k]
                op.val = self.dma_count[k]
                dma_last[k] = op
            else:
                if self.cnt[op.eng] >= 30000:
                    self.cnt[op.eng] = 0
                    self.nsem += 1
                    self.eng_sem[op.eng] = self.st.enter_context(
                        nc.semaphore("s_%s_%d" % (op.eng, self.nsem)))
                self.cnt[op.eng] += 1
                op.sem = self.eng_sem[op.eng]
                op.val = self.cnt[op.eng]
        for e in ENGS:
            seen = {}
            for op in self.eng_ops[e]:
                need = {}
                for d in op.deps:
                    if (not d.is_dma) and d.eng == e and d not in op.same:
                        continue
                    key = id(d.sem)
                    if seen.get(key, 0) >= d.val:
                        continue
                    if key not in need or need[key][1] < d.val:
                        need[key] = (d.sem, d.val)
                for key, (s, v) in need.items():
                    seen[key] = v
                    op.waits.append((s, v))
        fin = [(self.dma_sems[k], self.dma_count[k]) for k in range(self.N_DMA_SEMS)
               if dma_last[k] is not None]
        eng_ops = self.eng_ops
        with nc.Block() as block:
            def run(e, eng):
                for op in eng_ops[e]:
                    for (s, v) in op.waits:
                        eng.wait_ge(s, v)
                    ins = op.fn(eng)
                    if op.inc:
                        ins.then_inc(op.sem, 16 if op.is_dma else 1)

            @block.tensor
            def _(eng):
                run("pe", eng)

            @block.scalar
            def _(eng):
                run("act", eng)

            @block.vector
            def _(eng):
                run("dve", eng)

            @block.gpsimd
            def _(eng):
                run("pool", eng)

            @block.sync
            def _(eng):
                run("sp", eng)
                for (s, v) in fin:
                    eng.wait_ge(s, v)
        self._reset()


class Regs:
    def __init__(self, arena, ranges):
        self.arena = arena
        self.ranges = [[a, a, b] for (a, b) in ranges]

    def take(self, shape, dt):
        esz = 2 if dt == BF16 else 4
        n = 1
        for d_ in shape[1:]:
            n *= d_
        nbytes = (n * esz + 63) // 64 * 64
        for r in self.ranges:
            if r[0] + nbytes <= r[2]:
                a0 = r[0] // 2
                r[0] += nbytes
                ap = self.arena[0:shape[0], a0:a0 + n * esz // 2]
                if dt != BF16:
                    ap = ap.bitcast(dt)
                if len(shape) > 2:
                    names = "abcde"[:len(shape) - 1]
                    pat = "p (%s) -> p %s" % (" ".join(names), " ".join(names))
                    ap = ap.rearrange(pat, **{names[i]: shape[i + 1] for i in range(len(names))})
                return ap
        raise AssertionError("arena region overflow: %s %s" % (shape, self.ranges))


def build_nc(debug=False):
    nc = bass.Bass("TRN2", target_bir_lowering=False)

    def din(name, shape, dt=F32):
        return nc.dram_tensor(name, list(shape), dt, kind="ExternalInput").ap()

    xin = din("xin", [2, S, D])
    c3T = din("c3T", [128, DT, 4])
    w_mod = din("w_mod", [128, DT, 6 * D])
    bmodT = din("bmodT", [128, 48])
    n1T = din("n1T", [128, DT])
    n2T = din("n2T", [128, DT])
    fnw_d = din("fnw_b", [128, D])
    w_in = din("w_in", [128, DT, 4096])
    qnw_d = din("qnw_b", [128, 128])
    knw_d = din("knw_b", [128, 128])
    w_ab = din("w_ab", [128, DT, DT, 128])
    w_ing = din("w_ing", [128, 16, DT, 128])
    w_glu = din("w_glu", [128, 16, 4, 128])
    w_out = din("w_out", [128, DT, D])
    w_up = din("w_up", [128, 2 * NFT, DT, 128])
    w_down = din("w_down", [128, DT, NFT, 128])
    convT_d = din("convT", [128, 4, 2 * NFT])
    lamre_d = din("lamT_re", [128, 32])
    lamim_d = din("lamT_im", [128, 32])
    ldt_d = din("ldtT", [128, 32])
    Btre_d = din("Bt_re", [128, 32, 16])
    Btim_d = din("Bt_im", [128, 32, 16])
    Ctre_d = din("Ct_re", [128, 32, 16])
    Ctim_d = din("Ct_im", [128, 32, 16])
    dcol_d = din("dcol", [128, 32])
    ident_d = din("ident", [128, 128])
    ropec_d = din("ropec", [128, 16, 64])
    ropes_d = din("ropes", [128, 16, 64])
    kexp_d = din("kexp", [128, 32, 18])
    mask_d = din("mask", [128, 2, 128])
    out = nc.dram_tensor("out", [2, L, D], F32, kind="ExternalOutput").ap()
    skind = "ExternalOutput" if debug else "Internal"
    gt_scr = nc.dram_tensor("gt_scr", [128, 32, 2, 128], BF16, kind=skind).ap()
    hb_scr = nc.dram_tensor("hb_scr", [128, 2, 32, 128], BF16, kind=skind).ap()
    kw_scr = nc.dram_tensor("kw_scr", [128, 2, 32, 128], BF16, kind=skind).ap()

    if debug:
        dbg_xmid = nc.dram_tensor("dbg_xmid", [L, D], F32, kind="ExternalOutput").ap()
    with contextlib.ExitStack() as st:
        p = Prog(nc, st)

        uid = [0]

        def dump(name, ap, reads):
            if not debug:
                return
            d_ = nc.dram_tensor("dbg_" + name, list(ap.shape), ap.dtype, kind="ExternalOutput").ap()
            p.dma("sp", d_, ap, reads=list(reads), writes=["dbg_" + name])

        def SB(stack, name, shape, dt=F32):
            if isinstance(stack, Regs):
                return stack.take(list(shape), dt)
            uid[0] += 1
            return stack.enter_context(nc.sbuf_tensor("sb%d_%s" % (uid[0], name), list(shape), dt))

        psF = [st.enter_context(nc.psum_tensor("ps%d" % i, [128, 512], F32)) for i in range(8)]

        def PF(i):
            return psF[i][:]

        def PB(i):
            return psF[i][:].bitcast(BF16)

        def PK(i):
            return "ps%d" % i

        identf = SB(st, "identf", [128, 128])
        identb = SB(st, "identb", [128, 128], BF16)
        onesb = SB(st, "onesb", [128, 128], BF16)
        modT = SB(st, "modT", [128, 48, 4])
        scale1 = SB(st, "scale1", [128, DT, 4])
        scale2 = SB(st, "scale2", [128, DT, 4])
        qnw = SB(st, "qnw", [128, 128])
        knw = SB(st, "knw", [128, 128])
        convT = SB(st, "convT", [128, 4, 2 * NFT])
        CRI = SB(st, "CRI", [128, 2, 2, 32])
        CRt = CRI[:, 0]
        CIt = CRI[:, 1]
        n1s = SB(st, "n1s", [128, DT])
        n2s = SB(st, "n2s", [128, DT])

        with contextlib.ExitStack() as s0:
            for (t, d_, k) in ((identf, ident_d, "identf"), (qnw, qnw_d, "qnw"), (knw, knw_d, "knw"),
                               (convT, convT_d, "convT"), (n1s, n1T, "n1s"), (n2s, n2T, "n2s")):
                p.dma("sp", t[:], d_, writes=[k])
            p.dve(lambda e: e.tensor_copy(out=identb[:], in_=identf[:]), ["identf"], ["identb"])
            p.dve(lambda e: e.memset(onesb[:], 1.0), [], ["onesb"])

            c3 = SB(s0, "c3", [128, DT, 4])
            scT = SB(s0, "scT", [128, DT, 4])
            bmod = SB(s0, "bmod", [128, 48])
            p.dma("sp", c3[:], c3T, writes=["c3"])
            p.dma("sp", bmod[:], bmodT, writes=["bmod"])
            p.act(lambda e: e.activation(out=scT[:], in_=c3[:], func=AF.Silu), ["c3"], ["scT"])
            chain_rec = []
            real_add = p.add
            p.add = lambda *a, **k: chain_rec.append((a, k))
            def ld(name, shape, src):
                t = SB(s0, name, shape)
                p.dma("sp", t[:], src, writes=[name])
                return t
            lre = ld("lre", [128, 32], lamre_d)
            lim = ld("lim", [128, 32], lamim_d)
            ldt = ld("ldt", [128, 32], ldt_d)
            Btre = ld("Btre", [128, 32, 16], Btre_d)
            Btim = ld("Btim", [128, 32, 16], Btim_d)
            Ctre = ld("Ctre", [128, 32, 16], Ctre_d)
            Ctim = ld("Ctim", [128, 32, 16], Ctim_d)
            dcol = ld("dcol", [128, 32], dcol_d)
            kx = ld("kx", [128, 32, 18], kexp_d)
            msk = ld("msk", [128, 2, 128], mask_d)

            def tmp(name, shape, dt=F32):
                return SB(s0, name, shape, dt)

            def TT(o, a, b, op, r, w, eng="dve"):
                p.add(eng, lambda e: e.tensor_tensor(out=o, in0=a, in1=b, op=op), r, w)

            def TS(o, a, s1, s2, op0, op1, r, w, eng="dve"):
                if s2 is None:
                    p.add(eng, lambda e: e.tensor_scalar(out=o, in0=a, scalar1=s1, scalar2=None, op0=op0), r, w)
                else:
                    p.add(eng, lambda e: e.tensor_scalar(out=o, in0=a, scalar1=s1, scalar2=s2, op0=op0, op1=op1), r, w)

            dtv = tmp("dtv", [128, 32])
            a_ = tmp("a_", [128, 32])
            w_ = tmp("w_", [128, 32])
            p.act(lambda e: e.activation(out=dtv[:], in_=ldt[:], func=AF.Exp), ["ldt"], ["dtv"])
            TT(a_[:], lre[:], dtv[:], ALU.mult, ["lre", "dtv"], ["a_"])
            TT(w_[:], lim[:], dtv[:], ALU.mult, ["lim", "dtv"], ["w_"])
            magarg = tmp("magarg", [128, 32, 18])
            angarg = tmp("angarg", [128, 32, 18])
            mag = tmp("mag", [128, 32, 18])
            TT(magarg[:], a_[:].unsqueeze(2).to_broadcast([128, 32, 18]), kx[:], ALU.mult, ["a_", "kx"], ["magarg"])
            TT(angarg[:], w_[:].unsqueeze(2).to_broadcast([128, 32, 18]), kx[:], ALU.mult, ["w_", "kx"], ["angarg"])
            p.act(lambda e: e.activation(out=mag[:], in_=magarg[:], func=AF.Exp), ["magarg"], ["mag"])
            rf = tmp("rf", [128, 32, 18])
            ri_ = tmp("ri_", [128, 32, 18], I32)
            rk = tmp("rk", [128, 32, 18])
            red_s = tmp("red_s", [128, 32, 18])
            red_c = tmp("red_c", [128, 32, 18])
            sinv = tmp("sinv", [128, 32, 18])
            cosv = tmp("cosv", [128, 32, 18])

            def reduce_angle(dst, dk, add):
                TS(rf[:], angarg[:], 1.0 / TWO_PI, add / TWO_PI, ALU.mult, ALU.add, ["angarg"], ["rf"])
                p.dve(lambda e: e.tensor_copy(out=ri_[:], in_=rf[:]), ["rf"], ["ri_"])
                p.dve(lambda e: e.tensor_copy(out=rk[:], in_=ri_[:]), ["ri_"], ["rk"])
                p.dve(lambda e: e.scalar_tensor_tensor(out=dst[:], in0=rk[:], scalar=-TWO_PI, in1=angarg[:],
                                                       op0=ALU.mult, op1=ALU.add), ["rk", "angarg"], [dk])
                TS(dst[:], dst[:], add, None, ALU.add, None, [dk], [dk])
                TS(dst[:], dst[:], -PI_SAFE, PI_SAFE, ALU.max, ALU.min, [dk], [dk])
            reduce_angle(red_s, "red_s", 0.0)
            reduce_angle(red_c, "red_c", math.pi / 2)
            p.act(lambda e: e.activation(out=sinv[:], in_=red_s[:], func=AF.Sin), ["red_s"], ["sinv"])
            p.act(lambda e: e.activation(out=cosv[:], in_=red_c[:], func=AF.Sin), ["red_c"], ["cosv"])
            ApR = tmp("ApR", [128, 32, 18])
            ApI = tmp("ApI", [128, 32, 18])
            TT(ApR[:], mag[:], cosv[:], ALU.mult, ["mag", "cosv"], ["ApR"])
            TT(ApI[:], mag[:], sinv[:], ALU.mult, ["mag", "sinv"], ["ApI"])
            dump("ApR", ApR[:], ["ApR"])
            dump("ApI", ApI[:], ["ApI"])
            for r_ in range(2):
                p.dve(lambda e, r_=r_: e.tensor_copy(out=CRt[:, r_, :], in_=ApR[:, :, 17]), ["ApR"], ["CRt"])
            TS(CIt[:, 0, :], ApI[:, :, 17], -1.0, None, ALU.mult, None, ["ApI"], ["CIt"])
            p.dve(lambda e: e.tensor_copy(out=CIt[:, 1, :], in_=ApI[:, :, 17]), ["ApI"], ["CIt"])
            nr = tmp("nr", [128, 32])
            den = tmp("den", [128, 32])
            t32a = tmp("t32a", [128, 32])
            t32b = tmp("t32b", [128, 32])
            fre = tmp("fre", [128, 32])
            fim = tmp("fim", [128, 32])
            ni = ApI[:, :, 16]
            TS(nr[:], ApR[:, :, 16], -1.0, None, ALU.add, None, ["ApR"], ["nr"])
            TT(t32a[:], lre[:], lre[:], ALU.mult, ["lre"], ["t32a"])
            TT(den[:], lim[:], lim[:], ALU.mult, ["lim"], ["den"])
            TT(den[:], den[:], t32a[:], ALU.add, ["den", "t32a"], ["den"])
            p.dve(lambda e: e.reciprocal(out=den[:], in_=den[:]), ["den"], ["den"])
            TT(t32a[:], nr[:], lre[:], ALU.mult, ["nr", "lre"], ["t32a"])
            TT(t32b[:], ni, lim[:], ALU.mult, ["ApI", "lim"], ["t32b"])
            TT(t32a[:], t32a[:], t32b[:], ALU.add, ["t32a", "t32b"], ["t32a"])
            TT(fre[:], t32a[:], den[:], ALU.mult, ["t32a", "den"], ["fre"])
            TT(t32a[:], ni, lre[:], ALU.mult, ["ApI", "lre"], ["t32a"])
            TT(t32b[:], nr[:], lim[:], ALU.mult, ["nr", "lim"], ["t32b"])
            TT(t32a[:], t32a[:], t32b[:], ALU.subtract, ["t32a", "t32b"], ["t32a"])
            TT(fim[:], t32a[:], den[:], ALU.mult, ["t32a", "den"], ["fim"])
            Bbre = tmp("Bbre", [128, 32, 16])
            Bbim = tmp("Bbim", [128, 32, 16])
            tb1 = tmp("tb1", [128, 32, 16])
            tb2 = tmp("tb2", [128, 32, 16])
            bF = lambda t: t[:].unsqueeze(2).to_broadcast([128, 32, 16])
            TT(tb1[:], bF(fre), Btre[:], ALU.mult, ["fre", "Btre"], ["tb1"])
            TT(tb2[:], bF(fim), Btim[:], ALU.mult, ["fim", "Btim"], ["tb2"])
            TT(Bbre[:], tb1[:], tb2[:], ALU.subtract, ["tb1", "tb2"], ["Bbre"])
            TT(tb1[:], bF(fre), Btim[:], ALU.mult, ["fre", "Btim"], ["tb1"])
            TT(tb2[:], bF(fim), Btre[:], ALU.mult, ["fim", "Btre"], ["tb2"])
            TT(Bbim[:], tb1[:], tb2[:], ALU.add, ["tb1", "tb2"], ["Bbim"])
            dump("fre", fre[:], ["fre"])
            dump("fim", fim[:], ["fim"])
            dump("Bbre", Bbre[:], ["Bbre"])
            dump("Bbim", Bbim[:], ["Bbim"])
            G = tmp("G", [128, 2, 32, 128])
            Hf = tmp("Hf", [128, 2, 32, 128])
            tg1 = tmp("tg1", [128, 16, 8, 16])
            tg2 = tmp("tg2", [128, 16, 8, 16])
            for hv in range(2):
                gs = slice(hv * 16, hv * 16 + 16)

                def pw(tab, lo):
                    return tab[:, gs, lo:lo + 8].unsqueeze(3).to_broadcast([128, 16, 8, 16])

                def vec(t):
                    return t[:, gs, :].unsqueeze(2).to_broadcast([128, 16, 8, 16])

                def v4(t, r_):
                    return t[:, r_, gs, :].rearrange("p g (j q) -> p g j q", q=16)
                TT(tg1[:], pw(ApR, 0), vec(Bbre), ALU.mult, ["ApR", "Bbre"], ["tg1"])
                TT(tg2[:], pw(ApI, 0), vec(Bbim), ALU.mult, ["ApI", "Bbim"], ["tg2"])
                TT(v4(G, 0), tg1[:], tg2[:], ALU.subtract, ["tg1", "tg2"], ["G"])
                TT(tg1[:], pw(ApR, 0), vec(Bbim), ALU.mult, ["ApR", "Bbim"], ["tg1"])
                TT(tg2[:], pw(ApI, 0), vec(Bbre), ALU.mult, ["ApI", "Bbre"], ["tg2"])
                TT(v4(G, 1), tg1[:], tg2[:], ALU.add, ["tg1", "tg2"], ["G"])
                TT(tg1[:], pw(ApR, 8), vec(Ctre), ALU.mult, ["ApR", "Ctre"], ["tg1"])
                TT(tg2[:], pw(ApI, 8), vec(Ctim), ALU.mult, ["ApI", "Ctim"], ["tg2"])
                TT(v4(Hf, 0), tg1[:], tg2[:], ALU.subtract, ["tg1", "tg2"], ["Hf"])
                TT(tg1[:], pw(ApR, 8), vec(Ctim), ALU.mult, ["ApR", "Ctim"], ["tg1"])
                TT(tg2[:], pw(ApI, 8), vec(Ctre), ALU.mult, ["ApI", "Ctre"], ["tg2"])
                p.dve(lambda e, o_=v4(Hf, 1): e.scalar_tensor_tensor(out=o_, in0=tg1[:], scalar=-1.0, in1=tg2[:],
                                                                    op0=ALU.mult, op1=ALU.subtract), ["tg1", "tg2"], ["Hf"])
            dump("G", G[:], ["G"])
            p.add = real_add
            chain_ops = []
            for (a, k) in chain_rec:
                if k.get("dma") and len(a[2]) == 0:
                    real_add(*a, **k)
                else:
                    chain_ops.append((a, k))
            per_grp = (len(chain_ops) + 19) // 20

            def chain_some(n_):
                for _ in range(n_):
                    if chain_ops:
                        a, k = chain_ops.pop(0)
                        real_add(*a, **k)
            wm = [SB(s0, "wm%d" % i, [128, DT, 256]) for i in range(2)]
            modR = SB(s0, "modR", [4, 6 * D])
            for grp in range(24):
                w_ = wm[grp % 2]
                wk = "wm%d" % (grp % 2)
                bank = 1 + grp % 2
                p.dma("sp", w_[:], w_mod[:, :, grp * 256:(grp + 1) * 256], writes=[wk])
                for dt in range(DT):
                    p.pe(lambda e, w_=w_, dt=dt, bank=bank: e.matmul(
                        PF(bank)[0:4, 0:256], lhsT=scT[:, dt, :], rhs=w_[:, dt, :],
                        start=(dt == 0), stop=(dt == DT - 1)), [wk, "scT"], [PK(bank)])
                p.dve(lambda e, grp=grp, bank=bank: e.tensor_copy(out=modR[:, grp * 256:(grp + 1) * 256], in_=PF(bank)[0:4, 0:256]),
                      [PK(bank)], ["modR"])
                chain_some(per_grp)
            chain_some(len(chain_ops))
            for ft in range(48):
                p.pe(lambda e, ft=ft: e.transpose(out=PF(0)[:, ft * 4:(ft + 1) * 4], in_=modR[0:4, ft * 128:(ft + 1) * 128],
                                                  identity=identf[0:4, 0:4]), ["modR", "identf"], [PK(0)])
            p.dve(lambda e: e.tensor_tensor(
                out=modT[:], in0=PF(0)[:, 0:192].rearrange("p (f r) -> p f r", r=4),
                in1=bmod[:].unsqueeze(2).to_broadcast([128, 48, 4]), op=ALU.add),
                [PK(0), "bmod"], ["modT"])
            dump("modT", modT[:], ["modT"])
            mtmp = SB(s0, "mtmp", [128, DT, 4])
            for (sc, lo, nn, nk) in ((scale1, 8, n1s, "n1s"), (scale2, 32, n2s, "n2s")):
                p.dve(lambda e, lo=lo: e.tensor_scalar(out=mtmp[:], in0=modT[:, lo:lo + 8, :], scalar1=1.0,
                                                       scalar2=None, op0=ALU.add), ["modT"], ["mtmp"])
                p.dve(lambda e, sc=sc, nn=nn: e.tensor_tensor(
                    out=sc[:], in0=mtmp[:], in1=nn[:].unsqueeze(2).to_broadcast([128, DT, 4]), op=ALU.mult),
                    ["mtmp", nk], ["scales"])

            GTw = tmp("GTw", [128, 32, 2, 128], BF16)
            Kw = tmp("Kw", [128, 32, 128], BF16)
            p.dma("pool", hb_scr, Hf[:], reads=["Hf"], writes=["hb_scr"])
            cnt = 0
            for dr in range(2):
                for g0 in range(0, 32, 4):
                    bank = 1 + (cnt % 2)
                    cnt += 1
                    for s_ in range(4):
                        g = g0 + s_
                        gl, gh = g // 16, g % 16
                        gd = dr * 16 + gh
                        for r_ in range(2):
                            p.pe(lambda e, bank=bank, s_=s_, gl=gl, gd=gd, r_=r_: e.matmul(
                                PF(bank)[:, s_ * 128:(s_ + 1) * 128],
                                lhsT=G[gl * 64:(gl + 1) * 64, r_, gd, :], rhs=Hf[gl * 64:(gl + 1) * 64, r_, gd, :],
                                start=(r_ == 0), stop=(r_ == 1)), ["G", "Hf"], [PK(bank)])
                    p.dve(lambda e, bank=bank, dr=dr, g0=g0: e.tensor_tensor(
                        out=Kw[:, g0:g0 + 4, :], in0=PF(bank).rearrange("p (s q) -> p s q", q=128),
                        in1=msk[:, dr, :].unsqueeze(1).to_broadcast([128, 4, 128]), op=ALU.mult),
                        [PK(bank), "msk"], ["Kw"])
                if dr == 0:
                    for g in range(32):
                        p.dve(lambda e, g=g: e.scalar_tensor_tensor(
                            out=Kw[:, g, :], in0=identf[:], scalar=dcol[:, g:g + 1], in1=Kw[:, g, :],
                            op0=ALU.mult, op1=ALU.add), ["identf", "dcol", "Kw"], ["Kw"])
                p.dma("sp", kw_scr[:, dr, :, :], Kw[:], reads=["Kw"], writes=["kw_scr"])
            cnt = 0
            for gd0 in range(0, 32, 2):
                bank = 3 + (cnt % 2)
                cnt += 1
                for s_ in range(4):
                    gd, r_ = gd0 + s_ // 2, s_ % 2
                    p.pe(lambda e, bank=bank, s_=s_, gd=gd, r_=r_: e.transpose(
                        out=PF(bank)[:, s_ * 128:(s_ + 1) * 128], in_=G[:, r_, gd, :], identity=identf[:]),
                        ["G", "identf"], [PK(bank)])
                p.act(lambda e, bank=bank, gd0=gd0: e.activation(
                    out=GTw[:, gd0:gd0 + 2, :, :].rearrange("p a b c -> p (a b c)"), in_=PF(bank), func=AF.Identity),
                    [PK(bank)], ["GTw"])
            p.dma("sp", gt_scr, GTw[:], reads=["GTw"], writes=["gt_scr"])
            p.emit()

        def norm_tile(stk_bufs, src_ap, src_reads, dst_fn, dst_key, scale_t, shift_lo, r_idx, i, defer=None):
            nxs, nxn, nss = len(stk_bufs["xs"]), len(stk_bufs["xn"]), len(stk_bufs["ss"])
            xs, xsk = stk_bufs["xs"][i % nxs], "xs%d" % (i % nxs)
            xn, xnk = stk_bufs["xn"][i % nxn], "xn%d" % (i % nxn)
            ss, ssk = stk_bufs["ss"][i % nss], "ss%d" % (i % nss)
            junk = stk_bufs["junk"]
            if src_reads is None:
                p.dma("sp", xs[:], src_ap, writes=[xsk])
                xin_ap, rk_ = xs[:], [xsk]
            else:
                xin_ap, rk_ = src_ap, list(src_reads)
            p.act(lambda e: e.activation(out=junk[:], in_=xin_ap, func=AF.Square, accum_out=ss[:, 0:1]),
                  rk_, ["junk", ssk])
            p.act(lambda e: e.activation(out=ss[:, 2:3], in_=ss[:, 0:1], func=AF.Sqrt, scale=1.0 / D, bias=EPS),
                  [ssk], [ssk])
            p.dve(lambda e: e.reciprocal(out=ss[:, 3:4], in_=ss[:, 2:3]), [ssk], [ssk])
            p.act(lambda e: e.activation(out=xn[:], in_=xin_ap, func=AF.Identity, scale=ss[:, 3:4]),
                  rk_ + [ssk], [xnk])
            bank = 6 + (i % 2)

            def back():
                for dt in range(DT):
                    p.pe(lambda e, dt=dt: e.transpose(out=PB(bank)[:, dt * 128:(dt + 1) * 128],
                                                      in_=xn[:, dt * 128:(dt + 1) * 128], identity=identb[:]),
                         [xnk, "identb"], [PK(bank)])
                for dt in range(DT):
                    p.dve(lambda e, dt=dt: e.tensor_scalar(
                        out=dst_fn(dt), in0=PB(bank)[:, dt * 128:(dt + 1) * 128],
                        scalar1=scale_t[:, dt, r_idx:r_idx + 1], scalar2=modT[:, shift_lo + dt, r_idx:r_idx + 1],
                        op0=ALU.mult, op1=ALU.add), [PK(bank), "scales", "modT"], [dst_key], big=True)
            if defer is None:
                back()
            else:
                defer.append(back)

        K = 1024
        ARENA_BYTES = 188 * K
        arena = st.enter_context(nc.sbuf_tensor("arena", [128, ARENA_BYTES // 2], BF16))
        R_QM, R_HT, R_AT, R_UT, R_Z, R_KV, R_YS = ((0, 32 * K), (32 * K, 68 * K), (68 * K, 100 * K), (100 * K, 118 * K),
                                                    (118 * K, 154 * K), (154 * K, 172 * K), (172 * K, 188 * K))

        def RG(*ranges):
            return Regs(arena, ranges)
        qm = RG(R_QM).take([128, DT, L], BF16)
        hT = RG(R_HT).take([128, DT, S], BF16)
        attnT = RG(R_AT).take([128, NH, L], BF16)
        UT = RG(R_UT).take([128, 32, NCH], BF16)
        Z = RG(R_Z).take([128, 2, 32, NCH], BF16)
        rkv = RG(R_KV)
        kT = rkv.take([128, 2, S], BF16)
        V = rkv.take([128, NTT, 256], BF16)
        ysT = RG(R_YS).take([128, 4, L], BF16)
        mT = qm
        qT = qm
        h2T = qm

        for b in range(2):
            if True:
                if True:
                    s1 = RG(R_QM, R_KV, R_YS)
                    GTs = SB(s1, "GTs", [128, 32, 2, 128], BF16)
                    Utoks = [SB(s1, "Utok%d" % i, [128, 32, 8, 16], BF16) for i in range(2)]
                    bufs = {"xs": [SB(s1, "xsA%d" % i, [128, D]) for i in range(2)],
                            "xn": [SB(s1, "xnA%d" % i, [128, D], BF16) for i in range(3)],
                            "ss": [SB(s1, "ssA%d" % i, [128, 4]) for i in range(3)],
                            "junk": SB(s1, "junkA", [128, D], BF16)}
                    wu = SB(s1, "wu", [128, DT, 512], BF16)
                    p.dma("pool", wu[:], w_in[:, :, 1536:2048], writes=["wu"])
                    p.dma("sp", GTs[:], gt_scr, reads=["gt_scr"], writes=["GTs"])
                    tcount = [0]

                    nbacks = []

                    def s_norm(sp2, tl):
                        tt = sp2 * 8 + tl
                        r_idx = 2 if tt < 2 else b
                        norm_tile(bufs, xin[b, tt * 128:(tt + 1) * 128, :], None,
                                  lambda dt, tt=tt: hT[:, dt, tt * 128:(tt + 1) * 128], ("hT", tt),
                                  scale1, 0, r_idx, tcount[0], defer=nbacks)
                        tcount[0] += 1
                    for tl in range(8):
                        s_norm(0, tl)
                        if len(nbacks) > 1:
                            nbacks.pop(0)()
                    while nbacks:
                        nbacks.pop(0)()
                    for sp_ in range(3):
                        ntile = 8 if sp_ < 2 else 2
                        ncr = ntile * 16
                        hspan = hT[:, :, sp_ * 1024:sp_ * 1024 + ntile * 128]
                        Utok = Utoks[sp_ % 2]
                        utk = "Utok%d" % (sp_ % 2)
                        hkeys = [("hT", sp_ * 8 + tl) for tl in range(ntile)]
                        nnext = 0 if sp_ == 2 else (8 if sp_ + 1 < 2 else 2)
                        for j in range(8):
                            if j < nnext:
                                s_norm(sp_ + 1, j)
                            bank = j % 2
                            for dt in range(DT):
                                p.pe(lambda e, bank=bank, dt=dt, j=j, ncr=ncr, hspan=hspan: e.matmul(
                                    PF(bank)[0:ncr, :], lhsT=hspan[:, dt, j:j + 8 * (ncr - 1) + 1:8],
                                    rhs=wu[:, dt, :], start=(dt == 0), stop=(dt == DT - 1)),
                                    hkeys + ["wu"], [PK(bank)])
                            p.act(lambda e, bank=bank, j=j, ncr=ncr, Utok=Utok: e.activation(
                                out=Utok[0:ncr, :, j, :], in_=PF(bank)[0:ncr, :].rearrange("p (g q) -> p g q", q=16),
                                func=AF.Identity), [PK(bank)], [utk], big=True)
                            while len(nbacks) > 1:
                                nbacks.pop(0)()
                        while nbacks:
                            nbacks.pop(0)()
                        for g0 in range(0, 32, 8):
                            bank = 2 + (g0 // 8) % 2
                            for s_ in range(8):
                                g = g0 + s_
                                p.pe(lambda e, bank=bank, s_=s_, g=g, ncr=ncr, Utok=Utok: e.transpose(
                                    out=PB(bank)[:, s_ * 128:s_ * 128 + ncr],
                                    in_=Utok[0:ncr, g, :, :].rearrange("p j q -> p (j q)"),
                                    identity=identb[0:ncr, 0:ncr]), [utk, "identb"], [PK(bank)])
                            p.dve(lambda e, bank=bank, g0=g0, ncr=ncr, sp_=sp_: e.tensor_copy(
                                out=UT[:, g0:g0 + 8, sp_ * 128:sp_ * 128 + ncr],
                                in_=PB(bank).rearrange("p (s c) -> p s c", c=128)[:, :, 0:ncr]),
                                [PK(bank)], ["UT"], big=True)
                    if b == 0:
                        dump("UT", UT[:], ["UT"])
                    cnt = 0
                    for gd in range(32):
                        dr, gh = gd // 16, gd % 16
                        for r_ in range(2):
                            bank = 4 + (cnt % 2)
                            cnt += 1
                            for gl in range(2):
                                g = gl * 16 + gh
                                p.pe(lambda e, bank=bank, gd=gd, r_=r_, gl=gl, g=g: e.matmul(
                                    PF(bank)[gl * 64:(gl + 1) * 64, 0:NCH],
                                    lhsT=GTs[:, gd, r_, gl * 64:(gl + 1) * 64], rhs=UT[:, g, :],
                                    start=True, stop=True), ["GTs", "UT"], [PK(bank)])
                            if dr == 0:
                                p.act(lambda e, bank=bank, gd=gd, r_=r_: e.activation(
                                    out=Z[:, r_, gd, :], in_=PF(bank)[:, 0:NCH], func=AF.Identity), [PK(bank)], ["Z"], big=True)
                            else:
                                p.act(lambda e, bank=bank, gd=gd, r_=r_: e.activation(
                                    out=Z[:, r_, gd, 0:32], in_=PF(bank)[:, 31::-1], func=AF.Identity), [PK(bank)], ["Z"], big=True)
                                p.act(lambda e, bank=bank, gd=gd, r_=r_: e.activation(
                                    out=Z[:, r_, gd, 32:NCH], in_=PF(bank)[:, NCH - 1:31:-1], func=AF.Identity),
                                    [PK(bank)], ["Z"], big=True)
                    if b == 0:
                        dump("Z0", Z[:], ["Z"])
                    p.emit()
            if True:
                if True:
                    if True:
                        sQ = RG((R_YS[0], R_YS[0] + 8 * K))
                        s2 = sQ
                        X4 = [SB(s2, "X4_%d" % i, [128, 4, 32]) for i in range(2)]
                        Ts = [SB(s2, "sc_T_%d" % i, [128, 2, 2, 32]) for i in range(2)]
                        NU = 16
                        us = [SB(s2, "sc_u_%d" % i, [128, 2, 32]) for i in range(NU)]
                        pw_pending = []
                        pw_eng = ["act"]
                        PWD = 8
                        p.pool(lambda e: e.memset(X4[0][:], 0.0), [], ["X4_0"])

                        def win(v):
                            return bass.AP(v.tensor, v.offset, [list(v.ap[0]), [32, 2], [32, 2], [1, 32]])

                        def rep2(v):
                            pa = v.ap
                            return bass.AP(v.tensor, v.offset, [list(pa[0]), [0, 2], list(pa[1]), list(pa[2])])

                        def scan_step(s_):
                            xa, xb_ = X4[s_ % 2], X4[(s_ + 1) % 2]
                            ka, kb = "X4_%d" % (s_ % 2), "X4_%d" % ((s_ + 1) % 2)
                            T_, kT_ = Ts[s_ % 2], "sc_T_%d" % (s_ % 2)
                            u_, ku = us[s_ % NU], "sc_u_%d" % (s_ % NU)
                            p.pool(lambda e, xa=xa, T_=T_: e.tensor_tensor(out=T_[:], in0=CRI[:], in1=win(xa), op=ALU.mult),
                                   ["CRt", "CIt", ka], [kT_])
                            p.pool(lambda e, T_=T_, u_=u_: e.tensor_tensor(out=u_[:], in0=T_[:, 0], in1=T_[:, 1], op=ALU.add),
                                   [kT_], [ku])
                            p.pool(lambda e, xb_=xb_, u_=u_, s_=s_: e.tensor_tensor(
                                out=xb_[:].rearrange("p (a b) c -> p a b c", a=2), in0=rep2(u_), in1=rep2(Z[:, :, :, s_]),
                                op=ALU.add), [ku, ("Z", s_)], [kb])
                            if pw_eng[0] == "act":
                                pw_pending.append(lambda u_=u_, s_=s_, ku=ku: p.act(
                                    lambda e: e.activation(out=Z[:, :, :, s_], in_=u_[:], func=AF.Identity), [ku], [("Z", s_)]))
                            else:
                                pw_pending.append(lambda u_=u_, s_=s_, ku=ku: p.dve(
                                    lambda e: e.tensor_copy(out=Z[:, :, :, s_], in_=u_[:]), [ku], [("Z", s_)]))
                            if len(pw_pending) > PWD:
                                pw_pending.pop(0)()
                        NSC_B2 = 84
                        sQ = RG(R_AT, (R_YS[0] + 8 * K, R_YS[1]))
                        wg = [SB(sQ, "wg%d" % i, [128, DT, 512], BF16) for i in range(2)]
                        ropec = SB(sQ, "ropec", [128, 16, 64])
                        ropes = SB(sQ, "ropes", [128, 16, 64])
                        p.dma("sp", ropec[:], ropec_d, writes=["ropec"])
                        p.dma("sp", ropes[:], ropes_d, writes=["ropes"])
                        bufs = {"junk": SB(sQ, "junkB", [128, 128], BF16)}
                        junk4 = SB(sQ, "junk4", [128, 4, 128], BF16)
                        hss = [SB(sQ, "hss%d" % i, [128, 4, 4]) for i in range(3)]
                        for i in range(3):
                            p.dve(lambda e, i=i: e.memset(hss[i][:], 1.0), [], ["hss%d" % i])
                        qn = [SB(sQ, "qn%d" % i, [128, 4, 128]) for i in range(3)]
                        ra = SB(sQ, "ra", [128, 4, 64])
                        rb = SB(sQ, "rb", [128, 4, 64])
                        rc = SB(sQ, "rc", [128, 4, 64])
                        rd_ = SB(sQ, "rd_", [128, 4, 64])
                        qr = [SB(sQ, "qr%d" % i, [128, 4, 128], BF16) for i in range(3)]
                        it = 0
                        sc_next = [0]
                        pending = []
                        for gi, grp in enumerate((2, 0, 1)):
                            w_ = wg[gi % 2]
                            wk = "wg%d" % (gi % 2)
                            if gi == 0:
                                p.dma("pool", wg[0][:], w_in[:, :, 2 * 512:3 * 512], writes=["wg0"])
                                p.dma("pool", wg[1][:], w_in[:, :, 0:512], writes=["wg1"])
                            elif gi == 2:
                                p.dma("pool", w_[:], w_in[:, :, grp * 512:(grp + 1) * 512], writes=[wk])
                            nhd = 4 if grp < 2 else 2
                            nw, nwk = (qnw, "qnw") if grp < 2 else (knw, "knw")
                            for tt in range(2 if grp < 2 else 0, NTT):
                                bank = (0, 1, 4)[it % 3]
                                i2 = it % 3
                                it += 1
                                for _k in range(2):
                                    if sc_next[0] < NSC_B2:
                                        scan_step(sc_next[0])
                                        sc_next[0] += 1
                                for dt in range(DT):
                                    p.pe(lambda e, bank=bank, dt=dt, tt=tt, w_=w_: e.matmul(
                                        PF(bank), lhsT=hT[:, dt, tt * 128:(tt + 1) * 128], rhs=w_[:, dt, :],
                                        start=(dt == 0), stop=(dt == DT - 1)), [("hT", tt), wk], [PK(bank)])
                                hs, hk = hss[i2], "hss%d" % i2
                                q_, qk_ = qn[i2], "qn%d" % i2
                                qr_, qrk = qr[i2], "qr%d" % i2
                                hks = [hk + "_%d" % h_ for h_ in range(nhd)]
                                qks = [qk_ + "_%d" % h_ for h_ in range(nhd)]
                                for h_ in range(nhd):
                                    p.act(lambda e, bank=bank, h_=h_, hs=hs: e.activation(
                                        out=junk4[:, h_, :], in_=PF(bank)[:, h_ * 128:(h_ + 1) * 128],
                                        func=AF.Square, accum_out=hs[:, 0, h_:h_ + 1]), [PK(bank)], ["junk%d" % h_, hks[h_]])
                                p.act(lambda e, hs=hs: e.activation(out=hs[:, 2, :], in_=hs[:, 0, :], func=AF.Sqrt,
                                                                    scale=1.0 / 128, bias=EPS), hks, [hk])
                                p.dve(lambda e, hs=hs: e.reciprocal(out=hs[:, 3, :], in_=hs[:, 2, :]), [hk], [hk])
                                for h_ in range(nhd):
                                    p.dve(lambda e, bank=bank, h_=h_, hs=hs, q_=q_, nw=nw: e.scalar_tensor_tensor(
                                        out=q_[:, h_, :], in0=PF(bank)[:, h_ * 128:(h_ + 1) * 128],
                                        scalar=hs[:, 3, h_:h_ + 1], in1=nw[:], op0=ALU.mult, op1=ALU.mult),
                                        [PK(bank), hk, nwk], [qks[h_]])
                                if grp == 2:
                                    p.act(lambda e, bank=bank, tt=tt: e.activation(
                                        out=V[:, tt, :], in_=PF(bank)[:, 256:512], func=AF.Identity), [PK(bank)], [("V", tt)])
                                if tt >= 2:
                                    m = tt - 2
                                    ev = q_[:, 0:nhd, 0:128:2]
                                    od = q_[:, 0:nhd, 1:128:2]
                                    cs = ropec[:, m, :].unsqueeze(1).to_broadcast([128, nhd, 64])
                                    sn = ropes[:, m, :].unsqueeze(1).to_broadcast([128, nhd, 64])
                                    p.dve(lambda e, ev=ev, cs=cs, nhd=nhd: e.tensor_tensor(out=ra[:, 0:nhd, :], in0=ev, in1=cs, op=ALU.mult),
                                          qks + ["ropec"], ["ra"])
                                    p.dve(lambda e, od=od, sn=sn, nhd=nhd: e.tensor_tensor(out=rb[:, 0:nhd, :], in0=od, in1=sn, op=ALU.mult),
                                          qks + ["ropes"], ["rb"])
                                    p.dve(lambda e, ev=ev, sn=sn, nhd=nhd: e.tensor_tensor(out=rc[:, 0:nhd, :], in0=ev, in1=sn, op=ALU.mult),
                                          qks + ["ropes"], ["rc"])
                                    p.dve(lambda e, od=od, cs=cs, nhd=nhd: e.tensor_tensor(out=rd_[:, 0:nhd, :], in0=od, in1=cs, op=ALU.mult),
                                          qks + ["ropec"], ["rd_"])
                                    p.dve(lambda e, qr_=qr_, nhd=nhd: e.tensor_tensor(out=qr_[:, 0:nhd, 0:128:2], in0=ra[:, 0:nhd, :],
                                                                                      in1=rb[:, 0:nhd, :], op=ALU.subtract),
                                          ["ra", "rb"], [qrk])
                                    p.dve(lambda e, qr_=qr_, nhd=nhd: e.tensor_tensor(out=qr_[:, 0:nhd, 1:128:2], in0=rc[:, 0:nhd, :],
                                                                                      in1=rd_[:, 0:nhd, :], op=ALU.add),
                                          ["rc", "rd_"], [qrk + "o"])
                                else:
                                    p.pool(lambda e, qr_=qr_, q_=q_, nhd=nhd: e.tensor_copy(out=qr_[:, 0:nhd, :], in_=q_[:, 0:nhd, :]),
                                           qks, [qrk])
                                def back(it=it, qr_=qr_, qrk=qrk, nhd=nhd, grp=grp, tt=tt):
                                    tb_ = (2, 3, 5)[it % 3]
                                    for h_ in range(nhd):
                                        p.pe(lambda e, tb_=tb_, h_=h_, qr_=qr_: e.transpose(
                                            out=PB(tb_)[:, h_ * 128:(h_ + 1) * 128], in_=qr_[:, h_, :], identity=identb[:]),
                                            [qrk, qrk + "o", "identb"], [PK(tb_)])
                                    if grp < 2:
                                        m = tt - 2
                                        p.act(lambda e, tb_=tb_, grp=grp, m=m: e.activation(
                                            out=qT[:, grp * 4:(grp + 1) * 4, m * 128:(m + 1) * 128],
                                            in_=PB(tb_)[:, 0:512].rearrange("p (h c) -> p h c", c=128), func=AF.Copy),
                                            [PK(tb_)], [("qT", grp, m // 4)], big=True)
                                    else:
                                        p.act(lambda e, tb_=tb_, tt=tt: e.activation(
                                            out=kT[:, :, tt * 128:(tt + 1) * 128],
                                            in_=PB(tb_)[:, 0:256].rearrange("p (h c) -> p h c", c=128), func=AF.Copy),
                                            [PK(tb_)], [("kT", tt)])
                                pending.append(back)
                                if len(pending) > 1:
                                    pending.pop(0)()
                        while pending:
                            pending.pop(0)()
                        while pw_pending:
                            pw_pending.pop(0)()
                        if b == 0:
                            dump("hT", hT[:], [("hT", i) for i in range(NTT)])
                            dump("qT", qT[:], [("qT", g_, q_) for g_ in range(2) for q_ in range(4)])
                            dump("kT", kT[:], [("kT", i) for i in range(NTT)])
                            dump("V", V[:], [("V", i) for i in range(NTT)])
                        p.emit()
            if True:
                if True:
                    if True:
                        sQ = RG((R_YS[0] + 8 * K, R_YS[1]))
                        pw_eng[0] = "dve"
                        PT = [SB(sQ, "PT%d" % i, [128, 512], BF16) for i in range(4)]
                        rden = [SB(sQ, "rden%d" % i, [128, 512]) for i in range(2)]
                        sc_att = 1.0 / math.sqrt(128.0)
                        LA = 2
                        SB_ = [0, 1, 6, 7]
                        items = [(h_, qb, kt) for h_ in range(NH) for qb in range(4) for kt in range(NTT)]
                        for i in range(len(items) + LA):
                            if (i % 14) in (0, 3, 6, 9, 12) and sc_next[0] < NCH:
                                scan_step(sc_next[0])
                                sc_next[0] += 1
                            if i < len(items):
                                h_, qb, kt = items[i]
                                kvh = h_ // 4
                                bs = SB_[i % 4]
                                pt_, ptk = PT[i % 4], "PT%d" % (i % 4)
                                p.pe(lambda e, bs=bs, kt=kt, kvh=kvh, h_=h_, qb=qb: e.matmul(
                                    PF(bs), lhsT=kT[:, kvh, kt * 128:(kt + 1) * 128],
                                    rhs=qT[:, h_, qb * 512:(qb + 1) * 512], start=True, stop=True),
                                    [("kT", kt), ("qT", h_ // 4, qb)], [PK(bs)])
                                p.act(lambda e, bs=bs, pt_=pt_: e.activation(out=pt_[:], in_=PF(bs), func=AF.Exp,
                                                                             scale=sc_att), [PK(bs)], [ptk])
                            if i >= LA:
                                h_, qb, kt = items[i - LA]
                                kvh = h_ // 4
                                io = (h_ * 4 + qb) % 2
                                bo, bd = 2 + io, 4 + io
                                pt_, ptk = PT[(i - LA) % 4], "PT%d" % ((i - LA) % 4)
                                p.pe(lambda e, bo=bo, kt=kt, kvh=kvh, pt_=pt_: e.matmul(
                                    PF(bo), lhsT=V[:, kt, kvh * 128:(kvh + 1) * 128], rhs=pt_[:],
                                    start=(kt == 0), stop=(kt == NTT - 1)), [("V", kt), ptk], [PK(bo)])
                                p.pe(lambda e, bd=bd, kt=kt, pt_=pt_: e.matmul(
                                    PF(bd), lhsT=onesb[:], rhs=pt_[:],
                                    start=(kt == 0), stop=(kt == NTT - 1)), ["onesb", ptk], [PK(bd)])
                                if kt == NTT - 1:
                                    rd, rdk = rden[io], "rden%d" % io
                                    p.dve(lambda e, bd=bd, rd=rd: e.reciprocal(out=rd[:], in_=PF(bd)), [PK(bd)], [rdk])
                                    p.dve(lambda e, bo=bo, rd=rd, h_=h_, qb=qb: e.tensor_tensor(
                                        out=attnT[:, h_, qb * 512:(qb + 1) * 512], in0=PF(bo), in1=rd[:], op=ALU.mult),
                                        [PK(bo), rdk], [("attnT", qb)], big=True)
                        if b == 0:
                            dump("attnT", attnT[:], [("attnT", i) for i in range(4)])
                        while pw_pending:
                            pw_pending.pop(0)()
                        if b == 0:
                            dump("Z1", Z[:], [("Z", i) for i in range(NCH)])
                        p.emit()
            if True:
                if True:
                    s3 = RG(R_QM)
                    Hs = SB(s3, "Hs", [128, 2, 32, 128], BF16)
                    Ks = SB(s3, "Ks", [128, 2, 32, 128], BF16)
                    YT = RG(R_KV).take([128, 32, 256], BF16)
                    s3 = RG(R_Z, R_UT)
                    Ytok = SB(s3, "Ytok", [128, 2, 8, 512], BF16)
                    gys = [SB(s3, "gy%d" % i, [128, 1024]) for i in range(2)]
                    gts = [SB(s3, "gt_%d" % i, [128, 1024]) for i in range(2)]
                    gss = [SB(s3, "gs_%d" % i, [128, 1024]) for i in range(2)]
                    p.dma("sp", Hs[:], hb_scr, reads=["hb_scr"], writes=["Hs"])
                    p.dma("sp", Ks[:], kw_scr, reads=["kw_scr"], writes=["Ks"])
                    for g in range(32):
                        gl, gh = g // 16, g % 16
                        bank = g % 2
                        first = True
                        for dr in range(2):
                            gd = dr * 16 + gh
                            p.pe(lambda e, bank=bank, dr=dr, g=g, first=first: e.matmul(
                                PF(bank)[:, 0:256], lhsT=Ks[:, dr, g, :], rhs=UT[:, g, 32:NCH],
                                start=first, stop=False), ["Ks", "UT"], [PK(bank)])
                            first = False
                            for r_ in range(2):
                                last = (dr == 1 and r_ == 1)
                                p.pe(lambda e, bank=bank, gl=gl, gd=gd, r_=r_, last=last, dr=dr: e.matmul(
                                    PF(bank)[:, 0:256], lhsT=Hs[gl * 64:(gl + 1) * 64, r_, gd, :],
                                    rhs=(Z[gl * 64:(gl + 1) * 64, r_, gd, 32:NCH] if dr == 0
                                         else Z[gl * 64:(gl + 1) * 64, r_, gd, NCH - 1:31:-1]),
                                    start=False, stop=last), ["Hs", "Z"], [PK(bank)])
                        p.act(lambda e, bank=bank, g=g: e.activation(out=YT[:, g, :], in_=PF(bank)[:, 0:256],
                                                                      func=AF.Identity), [PK(bank)], ["YT"], big=True)
                    if b == 0:
                        dump("YT", YT[:], ["YT"])
                    p.emit()
                    cnt = 0
                    for ct2 in range(2):
                        for g0 in range(0, 32, 8):
                            bank = 2 + (cnt % 2)
                            cnt += 1
                            for s_ in range(8):
                                p.pe(lambda e, bank=bank, s_=s_, g0=g0, ct2=ct2: e.transpose(
                                    out=PB(bank)[:, s_ * 128:(s_ + 1) * 128],
                                    in_=YT[:, g0 + s_, ct2 * 128:(ct2 + 1) * 128], identity=identb[:]),
                                    ["YT", "identb"], [PK(bank)])
                            p.dve(lambda e, bank=bank, g0=g0, ct2=ct2: e.tensor_copy(
                                out=Ytok[:, ct2, :, g0 * 16:(g0 + 8) * 16].rearrange("p t (g q) -> p g t q", q=16),
                                in_=PB(bank).rearrange("p (g t q) -> p g t q", t=8, q=16)), [PK(bank)], ["Ytok"], big=True)
                    cnt = 0
                    for ct2 in range(2):
                        for chq in range(4):
                            bank = 4 + (cnt % 2)
                            cnt += 1
                            for tau in range(8):
                                p.pe(lambda e, bank=bank, tau=tau, ct2=ct2, chq=chq: e.transpose(
                                    out=PB(bank)[:, tau * 128:(tau + 1) * 128],
                                    in_=Ytok[:, ct2, tau, chq * 128:(chq + 1) * 128], identity=identb[:]),
                                    ["Ytok", "identb"], [PK(bank)])
                            gy, gt_, gs_ = gys[cnt % 2], gts[cnt % 2], gss[cnt % 2]
                            kgy, kgt, kgs = "gy%d" % (cnt % 2), "gt_%d" % (cnt % 2), "gs_%d" % (cnt % 2)
                            p.act(lambda e, bank=bank, gy=gy: e.activation(out=gy[:], in_=PB(bank), func=AF.Identity),
                                  [PK(bank)], [kgy])
                            p.dve(lambda e, gy=gy, gt_=gt_: e.tensor_tensor(out=gt_[:], in0=gy[:], in1=gy[:], op=ALU.mult), [kgy], [kgt])
                            p.dve(lambda e, gt_=gt_: e.tensor_scalar(out=gt_[:], in0=gt_[:], scalar1=0.044715, scalar2=1.0,
                                                                     op0=ALU.mult, op1=ALU.add), [kgt], [kgt], big=True)
                            p.dve(lambda e, gy=gy, gt_=gt_: e.tensor_tensor(out=gt_[:], in0=gt_[:], in1=gy[:], op=ALU.mult),
                                  [kgt, kgy], [kgt], big=True)
                            p.act(lambda e, gt_=gt_, gs_=gs_: e.activation(out=gs_[:], in_=gt_[:], func=AF.Sigmoid,
                                                                           scale=2.0 * math.sqrt(2.0 / math.pi)), [kgt], [kgs])
                            p.pool(lambda e, ct2=ct2, chq=chq, gy=gy, gs_=gs_: e.tensor_tensor(
                                out=ysT[:, chq, ct2 * 1024:(ct2 + 1) * 1024].rearrange("p (c t) -> p t c", t=8),
                                in0=gy[:].rearrange("p (t c) -> p t c", c=128),
                                in1=gs_[:].rearrange("p (t c) -> p t c", c=128), op=ALU.mult),
                                [kgy, kgs], ["ysT"], big=True)
                    if b == 0:
                        dump("ysT", ysT[:], ["ysT"])
                    p.emit()
            if True:
                if True:
                    if True:
                        sG = RG(R_UT, R_Z, R_KV)
                        wab = [SB(sG, "wab%d" % i, [128, DT, 128], BF16) for i in range(3)]
                        wga = [SB(sG, "wga%d" % i, [128, DT, 128], BF16) for i in range(3)]
                        wgs = [SB(sG, "wgs%d" % i, [128, DT, 128], BF16) for i in range(3)]
                        wla = [SB(sG, "wla%d" % i, [128, 4, 128], BF16) for i in range(3)]
                        wlb = [SB(sG, "wlb%d" % i, [128, 4, 128], BF16) for i in range(3)]

                        def issue_mw(ft_):
                            j_ = ft_ % 3
                            p.dma("pool", wab[j_][:], w_ab[:, ft_, :, :], writes=["wab%d" % j_])
                            p.dma("pool", wga[j_][:], w_ing[:, ft_, :, :], writes=["wga%d" % j_])
                            p.dma("pool", wgs[j_][:], w_ing[:, 8 + ft_, :, :], writes=["wgs%d" % j_])
                            p.dma("pool", wla[j_][:], w_glu[:, ft_, :, :], writes=["wla%d" % j_])
                            p.dma("pool", wlb[j_][:], w_glu[:, 8 + ft_, :, :], writes=["wlb%d" % j_])
                        issue_mw(0)
                        issue_mw(1)
                        sga = SB(sG, "sga", [128, 512])
                        sgs = SB(sG, "sgs", [128, 512])
                        sgb = SB(sG, "sgb", [128, 512])
                        m1 = SB(sG, "m1", [128, 512])
                        m2 = SB(sG, "m2", [128, 512])
                        bk = 0
                        for ft in range(DT):
                            i2 = ft % 3
                            ks = ["wab%d" % i2, "wga%d" % i2, "wgs%d" % i2, "wla%d" % i2, "wlb%d" % i2]
                            if ft + 2 < DT:
                                issue_mw(ft + 2)
                            for tb in range(4):
                                tok = slice(tb * 512, (tb + 1) * 512)
                                htok = slice(LC + tb * 512, LC + (tb + 1) * 512)
                                hkeys = [("hT", 2 + tb * 4 + i) for i in range(4)]
                                banks = [(bk + i) % 8 for i in range(5)]
                                bk += 5
                                b_pa, b_ga, b_gs, b_ua, b_ub = banks
                                for dt in range(DT):
                                    p.pe(lambda e, dt=dt, b_=b_pa, tok=tok, w=wab[i2]: e.matmul(
                                        PF(b_), lhsT=w[:, dt, :], rhs=attnT[:, dt, tok], start=(dt == 0), stop=(dt == DT - 1)),
                                        [ks[0], ("attnT", tb)], [PK(b_pa)])
                                for dt in range(DT):
                                    p.pe(lambda e, dt=dt, b_=b_ga, htok=htok, w=wga[i2]: e.matmul(
                                        PF(b_), lhsT=w[:, dt, :], rhs=hT[:, dt, htok], start=(dt == 0), stop=(dt == DT - 1)),
                                        [ks[1]] + hkeys, [PK(b_ga)])
                                for dt in range(DT):
                                    p.pe(lambda e, dt=dt, b_=b_gs, htok=htok, w=wgs[i2]: e.matmul(
                                        PF(b_), lhsT=w[:, dt, :], rhs=hT[:, dt, htok], start=(dt == 0), stop=(dt == DT - 1)),
                                        [ks[2]] + hkeys, [PK(b_gs)])
                                for k4 in range(4):
                                    p.pe(lambda e, k4=k4, b_=b_ua, tok=tok, w=wla[i2]: e.matmul(
                                        PF(b_), lhsT=w[:, k4, :], rhs=ysT[:, k4, tok], start=(k4 == 0), stop=(k4 == 3)),
                                        [ks[3], "ysT"], [PK(b_ua)])
                                for k4 in range(4):
                                    p.pe(lambda e, k4=k4, b_=b_ub, tok=tok, w=wlb[i2]: e.matmul(
                                        PF(b_), lhsT=w[:, k4, :], rhs=ysT[:, k4, tok], start=(k4 == 0), stop=(k4 == 3)),
                                        [ks[4], "ysT"], [PK(b_ub)])
                                p.act(lambda e, b_=b_ga: e.activation(out=sga[:], in_=PF(b_), func=AF.Sigmoid), [PK(b_ga)], ["sga"])
                                p.act(lambda e, b_=b_gs: e.activation(out=sgs[:], in_=PF(b_), func=AF.Sigmoid), [PK(b_gs)], ["sgs"])
                                p.act(lambda e, b_=b_ub: e.activation(out=sgb[:], in_=PF(b_), func=AF.Sigmoid), [PK(b_ub)], ["sgb"])
                                p.dve(lambda e, b_=b_pa: e.tensor_tensor(out=m1[:], in0=PF(b_), in1=sga[:], op=ALU.mult),
                                      [PK(b_pa), "sga"], ["m1"])
                                p.dve(lambda e, b_=b_ua: e.tensor_tensor(out=m2[:], in0=PF(b_), in1=sgb[:], op=ALU.mult),
                                      [PK(b_ua), "sgb"], ["m2"])
                                p.pool(lambda e: e.tensor_tensor(out=m2[:], in0=m2[:], in1=sgs[:], op=ALU.mult),
                                       ["m2", "sgs"], ["m2"])
                                p.pool(lambda e, ft=ft, tok=tok: e.tensor_tensor(out=mT[:, ft, tok], in0=m1[:], in1=m2[:], op=ALU.add),
                                       ["m1", "m2"], [("mT", tb)])
                        if b == 0:
                            dump("mT", mT[:], [("mT", i) for i in range(4)])
                        p.emit()
            if True:
                if True:
                    sO = RG(R_HT, R_AT, R_UT, R_Z)
                    xm = [SB(sO, "xm%d" % i, [128, 4, D]) for i in range(2)]
                    xg = [SB(sO, "xg%d" % i, [128, 512]) for i in range(2)]
                    wo = SB(sO, "wo", [128, DT, D], BF16)
                    xm.append(SB(sO, "xm2", [128, 4, D]))

                    def load_xm(tb_):
                        p.dma("sp", xm[tb_ % 3][:],
                              xin[b, LC + tb_ * 512:LC + (tb_ + 1) * 512, :].rearrange("(t p) d -> p t d", p=128),
                              writes=["xm%d" % (tb_ % 3)])
                    load_xm(0)
                    bufs2 = {"xs": [None, None],
                             "xn": [SB(sO, "xnD%d" % i, [128, D], BF16) for i in range(3)],
                             "ss": [SB(sO, "ssD%d" % i, [128, 4]) for i in range(3)],
                             "junk": SB(sO, "junkD", [128, D], BF16)}
                    g1b = SB(sO, "g1b", [128, D])
                    gTs = SB(sO, "gTs", [8, 128])
                    Esel = SB(sO, "Esel", [8, 8, 128])
                    p.dma("pool", wo[:], w_out, writes=["wo"])
                    p.pe(lambda e: e.transpose(out=PF(2)[0:8, 0:128], in_=modT[:, 16:24, b], identity=identf[:]),
                         ["modT", "identf"], [PK(2)])
                    p.act(lambda e: e.activation(out=gTs[:], in_=PF(2)[0:8, 0:128], func=AF.Identity), [PK(2)], ["gTs"])
                    for ft in range(DT):
                        p.dve(lambda e, ft=ft: e.tensor_copy(out=Esel[:, ft, :], in_=identf[0:8, ft:ft + 1].to_broadcast([8, 128])),
                              ["identf"], ["Esel"])
                    for ft in range(DT):
                        bk_ = 4 + ft // 4
                        p.pe(lambda e, ft=ft, bk_=bk_: e.matmul(PF(bk_)[:, (ft % 4) * 128:(ft % 4 + 1) * 128], lhsT=Esel[:, ft, :],
                                                                 rhs=gTs[:], start=True, stop=True), ["Esel", "gTs"], [PK(bk_)])
                    for hh in range(2):
                        p.act(lambda e, hh=hh: e.activation(out=g1b[:, hh * 512:(hh + 1) * 512], in_=PF(4 + hh), func=AF.Identity),
                              [PK(4 + hh)], ["g1b"])
                    ig = 0
                    norm_pending = []
                    n2backs = []
                    for tb in range(4):
                        xm_, xmk = xm[tb % 3], "xm%d" % (tb % 3)
                        if tb + 1 < 4:
                            load_xm(tb + 1)
                        for t4 in range(4):
                            while len(n2backs) > 1:
                                n2backs.pop(0)()
                            if norm_pending:
                                norm_pending.pop(0)()
                            tt_ = tb * 4 + t4
                            for hh in range(2):
                                bank = ig % 2
                                xg_, xgk = xg[ig % 2], "xg%d" % (ig % 2)
                                ig += 1
                                for dt in range(DT):
                                    p.pe(lambda e, bank=bank, dt=dt, tt_=tt_, hh=hh: e.matmul(
                                        PF(bank), lhsT=mT[:, dt, tt_ * 128:(tt_ + 1) * 128], rhs=wo[:, dt, hh * 512:(hh + 1) * 512],
                                        start=(dt == 0), stop=(dt == DT - 1)), ["wo", ("mT", tb)], [PK(bank)])
                                p.dve(lambda e, bank=bank, xg_=xg_, hh=hh: e.tensor_tensor(
                                    out=xg_[:], in0=PF(bank), in1=g1b[:, hh * 512:(hh + 1) * 512], op=ALU.mult),
                                    [PK(bank), "g1b"], [xgk])
                                p.pool(lambda e, xg_=xg_, xm_=xm_, t4=t4, hh=hh: e.tensor_tensor(
                                    out=xm_[:, t4, hh * 512:(hh + 1) * 512], in0=xg_[:], in1=xm_[:, t4, hh * 512:(hh + 1) * 512],
                                    op=ALU.add), [xgk, xmk], [xmk], big=True)
                        p.dma("sp", out[b, tb * 512:(tb + 1) * 512, :].rearrange("(t p) d -> p t d", p=128), xm_[:],
                              reads=[xmk], writes=["outscr"])
                        if b == 0 and debug:
                            p.dma("sp", dbg_xmid[tb * 512:(tb + 1) * 512, :].rearrange("(t p) d -> p t d", p=128), xm_[:],
                                  reads=[xmk], writes=["dbg_xmid"])

                        for t4 in range(4):
                            def norm2(tb=tb, xm_=xm_, xmk=xmk, t4=t4):
                                tt_ = tb * 4 + t4
                                norm_tile(bufs2, xm_[:, t4, :], [xmk],
                                          lambda dt, tt_=tt_: qm[:, dt, tt_ * 128:(tt_ + 1) * 128], ("mT", tb),
                                          scale2, 24, b, tt_, defer=n2backs)
                            norm_pending.append(norm2)
                    while norm_pending or n2backs:
                        while n2backs:
                            n2backs.pop(0)()
                        if norm_pending:
                            norm_pending.pop(0)()
                    p.emit()
            if True:
                if True:
                    sF = RG((32 * K, 188 * K))
                    fnw = SB(sF, "fnw", [128, D])
                    p.dma("sp", fnw[:], fnw_d, writes=["fnw"])
                    aT = SB(sF, "aT", [128, NFT, 1024], BF16)
                    NWU = 6
                    wup = [SB(sF, "wup%d" % i, [128, DT, 128], BF16) for i in range(NWU)]
                    wdn = [SB(sF, "wdn%d" % i, [128, NFT, 128], BF16) for i in range(3)]
                    zb = [SB(sF, "zb%d" % i, [128, 1026]) for i in range(2)]
                    cgs = [SB(sF, "cg%d" % i, [128, 1024]) for i in range(4)]
                    sgs_ = [SB(sF, "sg_%d" % i, [128, 1024], BF16) for i in range(2)]
                    cg = cgs[0]
                    yg = [SB(sF, "yg%d" % i, [128, 512]) for i in range(2)]
                    xo = [SB(sF, "xo%d" % i, [128, 4, D]) for i in range(2)]
                    ot = [SB(sF, "ot%d" % i, [128, D]) for i in range(2)]
                    fss = [SB(sF, "fss%d" % i, [128, 4]) for i in range(2)]
                    iw = 0
                    iz = 0
                    for hf in range(2):
                        t0 = hf * 1024
                        tiles = [(ft, which) for ft in range(NFT) for which in range(2)]

                        def issue_wup(idx, iw0):
                            ft_, wh_ = tiles[idx]
                            fc_ = wh_ * NFT + ft_
                            k_ = (iw0 + idx) % NWU
                            p.dma("pool", wup[k_][:], w_up[:, fc_, :, :], writes=["wup%d" % k_])
                        iw0 = iw
                        ffn_pending = []
                        for zi in range(2):
                            zc_ = 0 if hf == 0 else 1025
                            p.pool(lambda e, zi=zi, zc_=zc_: e.memset(zb[zi][:, zc_:zc_ + 1], 0.0), [], ["zb%d" % zi])
                        if hf == 0:
                            for idx in range(4):
                                issue_wup(idx, iw0)
                        for ft in range(NFT):
                            if ft == NFT - 6:
                                for m2_ in range(2):
                                    p.dma("pool", wdn[m2_][:], w_down[:, m2_, :, :], writes=["wdn%d" % m2_])
                                for t2 in range(2):
                                    p.dma("sp", xo[t2][:],
                                          out[b, t0 + t2 * 512:t0 + (t2 + 1) * 512, :].rearrange("(t p) d -> p t d", p=128),
                                          reads=["outscr"], writes=["xo%d" % t2])
                            for which in range(2):
                                fcol = which * NFT + ft
                                w_, wk = wup[iw % NWU], "wup%d" % (iw % NWU)
                                if iw - iw0 + 4 < len(tiles):
                                    issue_wup(iw - iw0 + 4, iw0)
                                iw += 1
                                z_, zk = zb[iz % 2], "zb%d" % (iz % 2)
                                iz += 1
                                b0_ = (iz % 2) * 3
                                for t2 in range(2):
                                    bank = b0_ + t2
                                    for dt in range(DT):
                                        p.pe(lambda e, bank=bank, dt=dt, w_=w_, t2=t2, t0=t0: e.matmul(
                                            PF(bank), lhsT=w_[:, dt, :], rhs=h2T[:, dt, t0 + t2 * 512:t0 + (t2 + 1) * 512],
                                            start=(dt == 0), stop=(dt == DT - 1)),
                                            [wk, ("h2T", hf * 2 + t2)], [PK(bank)])
                                    p.act(lambda e, bank=bank, z_=z_, t2=t2: e.activation(
                                        out=z_[:, 1 + t2 * 512:1 + (t2 + 1) * 512], in_=PF(bank), func=AF.Identity),
                                        [PK(bank)], [zk], big=True)
                                hb_ = b0_ + 2
                                hcol = (t0 + 1024) if hf == 0 else (t0 - 1)
                                for dt in range(DT):
                                    p.pe(lambda e, hb_=hb_, dt=dt, w_=w_, hcol=hcol: e.matmul(
                                        PF(hb_)[:, 0:1], lhsT=w_[:, dt, :], rhs=h2T[:, dt, hcol:hcol + 1],
                                        start=(dt == 0), stop=(dt == DT - 1)),
                                        [wk, ("h2T", (hcol // 512))], [PK(hb_)])
                                if hf == 0:
                                    p.act(lambda e, hb_=hb_, z_=z_: e.activation(out=z_[:, 1025:1026], in_=PF(hb_)[:, 0:1],
                                                                                 func=AF.Identity), [PK(hb_)], [zk], big=True)
                                else:
                                    p.act(lambda e, hb_=hb_, z_=z_: e.activation(out=z_[:, 0:1], in_=PF(hb_)[:, 0:1],
                                                                                 func=AF.Identity), [PK(hb_)], [zk], big=True)
                                cw = lambda j, fcol=fcol: convT[:, j, fcol:fcol + 1]
                                cg, cgk = cgs[iz % 4], "cg%d" % (iz % 4)
                                sg_, sgk = sgs_[(iz // 2) % 2], "sg_%d" % ((iz // 2) % 2)
                                p.act(lambda e, z_=z_, cw=cw, cg=cg: e.activation(out=cg[:], in_=z_[:, 1:1025], func=AF.Identity,
                                                                                  scale=cw(1), bias=cw(3)), [zk, "convT"], [cgk], big=True)
                                p.dve(lambda e, z_=z_, cw=cw, cg=cg: e.scalar_tensor_tensor(out=cg[:], in0=z_[:, 0:1024], scalar=cw(0),
                                                                                            in1=cg[:], op0=ALU.mult, op1=ALU.add),
                                      [zk, "convT", cgk], [cgk])
                                p.dve(lambda e, z_=z_, cw=cw, cg=cg: e.scalar_tensor_tensor(
                                    out=cg[:], in0=z_[:, 2:1026], scalar=cw(2), in1=cg[:], op0=ALU.mult, op1=ALU.add),
                                    [zk, "convT", cgk], [cgk], big=True)
                                if which == 0:
                                    cg_val, cgk_val = cg, cgk
                                else:
                                    def gate_tail(cg=cg, sg_=sg_, cgk=cgk, sgk=sgk, ft=ft, cg_val=cg_val, cgk_val=cgk_val):
                                        p.act(lambda e, cg=cg, sg_=sg_: e.activation(out=sg_[:], in_=cg[:], func=AF.Silu), [cgk], [sgk])
                                        p.pool(lambda e, ft=ft, sg_=sg_, cg_val=cg_val: e.tensor_tensor(
                                            out=aT[:, ft, :], in0=cg_val[:], in1=sg_[:], op=ALU.mult),
                                            [cgk_val, sgk], [("aT", ft)])
                                    ffn_pending.append(gate_tail)
                                if which == 0 and ffn_pending:
                                    ffn_pending.pop(0)()
                        while ffn_pending:
                            ffn_pending.pop(0)()
                        if hf == 0:
                            for idx in range(4):
                                issue_wup(idx, iw)
                        ig = 0
                        dn_pending = []
                        for mt in range(DT):
                            wd_, wdk = wdn[mt % 3], "wdn%d" % (mt % 3)
                            if mt + 2 < DT:
                                p.dma("pool", wdn[(mt + 2) % 3][:], w_down[:, mt + 2, :, :], writes=["wdn%d" % ((mt + 2) % 3)])
                            for t2 in range(2):
                                bank = 6 + ig % 2
                                tbk = 2 + 3 * (ig % 2)
                                yg_, ygk = yg[ig % 2], "yg%d" % (ig % 2)
                                ig += 1
                                for k in range(NFT):
                                    p.pe(lambda e, bank=bank, k=k, wd_=wd_, t2=t2: e.matmul(
                                        PF(bank), lhsT=wd_[:, k, :], rhs=aT[:, k, t2 * 512:(t2 + 1) * 512],
                                        start=(k == 0), stop=(k == NFT - 1)), [wdk, ("aT", k)], [PK(bank)])
                                p.act(lambda e, bank=bank, yg_=yg_, mt=mt: e.activation(
                                    out=yg_[:], in_=PF(bank), func=AF.Identity, scale=modT[:, 40 + mt, b:b + 1]),
                                    [PK(bank), "modT"], [ygk])
                                def dn_tail(tbk=tbk, yg_=yg_, ygk=ygk, t2=t2, mt=mt):
                                    for t4 in range(4):
                                        p.pe(lambda e, tbk=tbk, t4=t4, yg_=yg_: e.transpose(
                                            out=PF(tbk)[:, t4 * 128:(t4 + 1) * 128], in_=yg_[:, t4 * 128:(t4 + 1) * 128],
                                            identity=identf[:]), [ygk, "identf"], [PK(tbk)])
                                    xo_, xok = xo[t2], "xo%d" % t2
                                    p.dve(lambda e, tbk=tbk, xo_=xo_, mt=mt: e.tensor_tensor(
                                        out=xo_[:, :, mt * 128:(mt + 1) * 128], in0=PF(tbk).rearrange("p (t c) -> p t c", c=128),
                                        in1=xo_[:, :, mt * 128:(mt + 1) * 128], op=ALU.add), [PK(tbk), xok], [xok])
                                dn_pending.append(dn_tail)
                                if len(dn_pending) > 1:
                                    dn_pending.pop(0)()
                        while dn_pending:
                            dn_pending.pop(0)()
                        io_ = 0
                        for t2 in range(2):
                            xo_, xok = xo[t2], "xo%d" % t2
                            for t4 in range(4):
                                fs, fsk = fss[io_ % 2], "fss%d" % (io_ % 2)
                                o_, ok_ = ot[io_ % 2], "ot%d" % (io_ % 2)
                                io_ += 1
                                p.act(lambda e, xo_=xo_, t4=t4, fs=fs: e.activation(
                                    out=cgs[0][:], in_=xo_[:, t4, :], func=AF.Square, accum_out=fs[:, 0:1]),
                                    [xok], ["cg0", fsk])
                                p.act(lambda e, fs=fs: e.activation(out=fs[:, 2:3], in_=fs[:, 0:1], func=AF.Sqrt,
                                                                    scale=1.0 / D, bias=EPS), [fsk], [fsk])
                                p.dve(lambda e, fs=fs: e.reciprocal(out=fs[:, 3:4], in_=fs[:, 2:3]), [fsk], [fsk])
                                p.dve(lambda e, xo_=xo_, t4=t4, fs=fs, o_=o_: e.scalar_tensor_tensor(
                                    out=o_[:], in0=xo_[:, t4, :], scalar=fs[:, 3:4], in1=fnw[:], op0=ALU.mult, op1=ALU.mult),
                                    [xok, fsk, "fnw"], [ok_])
                                r0 = t0 + t2 * 512 + t4 * 128
                                p.dma("sp", out[b, r0:r0 + 128, :], o_[:], reads=[ok_], writes=["outfin"])
                    p.emit()
    return nc


def _pm(w, kt):
    w = np.asarray(w, dtype=np.float32)
    return np.ascontiguousarray(w.reshape(kt, 128, w.shape[1]).transpose(1, 0, 2))


def _pmt(w, kt):
    w = np.asarray(w, dtype=np.float32)
    nt = w.shape[1] // 128
    return np.ascontiguousarray(w.reshape(kt, 128, nt, 128).transpose(1, 2, 0, 3))


def _glT(a):
    a = np.asarray(a, dtype=np.float32)
    rest = a.shape[3:]
    a = a.reshape((2, 2, 16, 64) + rest)
    a = np.moveaxis(a, (1, 3, 0, 2), (0, 1, 2, 3))
    return np.ascontiguousarray(a.reshape((128, 32) + rest))


_NC_CACHE = {}


def kernel(x, c, ctx, c_ctx, w_mod, b_mod, norm1_w, norm2_w, w_in, q_norm_w, k_norm_w,
           w_attn_br, ssm_lambda_re, ssm_lambda_im, ssm_log_dt, ssm_b_re, ssm_b_im,
           ssm_c_re, ssm_c_im, ssm_d, w_glu, w_out, w_up, conv_w, conv_b, w_down,
           final_norm_w):
    f32 = np.float32
    x = np.asarray(x, f32)
    ctx = np.asarray(ctx, f32)
    c = np.asarray(c, f32)
    c_ctx = np.asarray(c_ctx, f32)
    n_cores = 8
    ident = np.eye(128, dtype=f32)
    inv_freq = (10000.0 ** (-np.arange(0, 64, 2, dtype=np.float32) / 64.0)).astype(f32)
    t = np.arange(L)
    rows = (t // 64).astype(f32)
    cols = (t % 64).astype(f32)
    ang = np.concatenate([rows[:, None] * inv_freq, cols[:, None] * inv_freq], axis=-1).astype(f32)
    ropec = np.ascontiguousarray(np.cos(ang).astype(f32).reshape(16, 128, 64).transpose(1, 0, 2))
    ropes = np.ascontiguousarray(np.sin(ang).astype(f32).reshape(16, 128, 64).transpose(1, 0, 2))
    kexp = np.zeros((128, 32, 18), f32)
    j8 = np.arange(8, dtype=f32)
    kexp[:, 0:16, 0:8] = -j8
    kexp[:, 0:16, 8:16] = j8
    kexp[:, 16:32, 0:8] = j8 - 7.0
    kexp[:, 16:32, 8:16] = 7.0 - j8
    kexp[:, :, 16] = 1.0
    kexp[:, :, 17] = 8.0
    jj = np.repeat(np.arange(8), 16)
    mask = np.zeros((128, 2, 128), f32)
    mask[:, 0, :] = (jj[None, :] >= jj[:, None])
    mask[:, 1, :] = (jj[None, :] <= jj[:, None])

    shared = {
        "w_mod": _pm(w_mod[0], 8),
        "bmodT": np.ascontiguousarray(np.asarray(b_mod[0], f32).reshape(48, 128).T),
        "n1T": np.ascontiguousarray(np.asarray(norm1_w[0], f32).reshape(8, 128).T),
        "n2T": np.ascontiguousarray(np.asarray(norm2_w[0], f32).reshape(8, 128).T),
        "fnw_b": np.ascontiguousarray(np.broadcast_to(np.asarray(final_norm_w, f32)[None, :], (128, D))),
        "w_in": _pm(w_in[0], 8),
        "qnw_b": np.ascontiguousarray(np.broadcast_to(np.asarray(q_norm_w[0], f32)[None, :], (128, 128))),
        "knw_b": np.ascontiguousarray(np.broadcast_to(np.asarray(k_norm_w[0], f32)[None, :], (128, 128))),
        "w_ab": _pmt(w_attn_br[0], 8),
        "w_ing": _pmt(np.asarray(w_in[0], np.float32)[:, 2048:], 8),
        "w_glu": _pmt(w_glu[0], 4),
        "w_out": _pm(w_out[0], 8),
        "w_up": _pmt(w_up[0], 8),
        "w_down": _pmt(w_down[0], NFT),
        "convT": np.ascontiguousarray(np.concatenate(
            [np.asarray(conv_w[0], f32), np.asarray(conv_b[0], f32)[None, :]], axis=0
        ).reshape(4, 2 * NFT, 128).transpose(2, 0, 1)),
        "lamT_re": _glT(ssm_lambda_re[0]),
        "lamT_im": _glT(ssm_lambda_im[0]),
        "ldtT": _glT(np.broadcast_to(np.asarray(ssm_log_dt[0], f32)[:, :, None], (2, 32, 64))),
        "Bt_re": _glT(ssm_b_re[0]),
        "Bt_im": _glT(ssm_b_im[0]),
        "Ct_re": _glT(np.asarray(ssm_c_re[0], f32).transpose(0, 1, 3, 2)),
        "Ct_im": _glT(np.asarray(ssm_c_im[0], f32).transpose(0, 1, 3, 2)),
        "dcol": np.ascontiguousarray(np.broadcast_to(
            np.asarray(ssm_d[0], f32).reshape(32, 16).T[None, :, :], (8, 16, 32)).reshape(128, 32)),
        "ident": ident, "ropec": ropec, "ropes": ropes, "kexp": kexp, "mask": mask,
    }
    in_maps = []
    for i in range(n_cores):
        bs = slice(2 * i, 2 * i + 2)
        xin = np.ascontiguousarray(np.concatenate([ctx[bs], x[bs]], axis=1))
        c3 = np.concatenate([c[bs], c_ctx[None, :], np.zeros((1, D), f32)], axis=0)
        c3T = np.ascontiguousarray(c3.reshape(4, 8, 128).transpose(2, 1, 0))
        m = dict(shared)
        m["xin"] = xin
        m["c3T"] = c3T
        in_maps.append(m)
    if _NC_CACHE.get("debug"):
        nc = build_nc(debug=True)
        res = run_bass_kernel_spmd(nc, in_maps[:1], core_ids=[0])
        _NC_CACHE["dbg_res"] = res.results[0]
        _NC_CACHE["dbg_in"] = in_maps[0]
        return None
    if "nc" not in _NC_CACHE:
        _NC_CACHE["nc"] = build_nc()
    nc = _NC_CACHE["nc"]
    res = run_bass_kernel_spmd(nc, in_maps, core_ids=list(range(n_cores)))
    outs = [np.asarray(r["out"], dtype=f32) for r in res.results]
    return np.concatenate(outs, axis=0)
```
